# Optimizing a Trainium2 kernel written in Bass

```python
import jax, jax.numpy as jnp
from jax import lax
import numpy as np

D_MODEL = 2048
BATCH = 8
SEQ = 2048
DEPTH = 1

HEAD_DIM = 64
FOX_HEADS = D_MODEL // (2 * HEAD_DIM)
SWA_Q_HEADS = D_MODEL // (2 * HEAD_DIM)
SWA_KV_HEADS = SWA_Q_HEADS // 4
D_MIX = (FOX_HEADS + SWA_Q_HEADS) * HEAD_DIM
WINDOW = 128
SWA_BLOCK = WINDOW
Q_BLOCK = 128
D_FF = 5632
ROPE_THETA = 10000.0
EPS = 1e-6

SPLIT_SIZES = (
    FOX_HEADS * HEAD_DIM,
    FOX_HEADS * HEAD_DIM,
    FOX_HEADS * HEAD_DIM,
    FOX_HEADS,
    SWA_Q_HEADS * HEAD_DIM,
    SWA_KV_HEADS * HEAD_DIM,
    SWA_KV_HEADS * HEAD_DIM,
)
D_IN_PROJ = sum(SPLIT_SIZES)
SPLIT_POINTS = tuple(int(v) for v in np.cumsum(SPLIT_SIZES)[:-1])

kernel_name = "hybrid_fox_swa_sink_macaron"


def rms_norm(x, g):
    xf = x.astype(jnp.float32)
    y = xf * lax.rsqrt(jnp.mean(xf * xf, axis=-1, keepdims=True) + EPS)
    return (y * g.astype(jnp.float32)).astype(x.dtype)


def swiglu(x, w_gate, w_up, w_down):
    return (jax.nn.silu(x @ w_gate) * (x @ w_up)) @ w_down


def rope(x, positions):
    d = x.shape[-1]
    inv_freq = ROPE_THETA ** (-jnp.arange(0, d, 2, dtype=jnp.float32) / d)
    ang = positions.astype(jnp.float32)[..., None] * inv_freq
    cos = jnp.cos(ang)[:, :, None, :]
    sin = jnp.sin(ang)[:, :, None, :]
    x1, x2 = jnp.split(x.astype(jnp.float32), 2, axis=-1)
    out = jnp.concatenate([x1 * cos - x2 * sin, x2 * cos + x1 * sin], axis=-1)
    return out.astype(x.dtype)


def forgetting_attention(q, k, v, log_f):
    B, S, H, d = q.shape
    nb = S // Q_BLOCK
    c = jnp.cumsum(log_f, axis=1).transpose(0, 2, 1)
    qh = q.transpose(0, 2, 1, 3)
    kh = k.transpose(0, 2, 1, 3)
    vh = v.transpose(0, 2, 1, 3)
    qb = qh.reshape(B, H, nb, Q_BLOCK, d).transpose(2, 0, 1, 3, 4)
    cb = c.reshape(B, H, nb, Q_BLOCK).transpose(2, 0, 1, 3)
    q_pos = jnp.arange(S).reshape(nb, Q_BLOCK)
    k_pos = jnp.arange(S)
    scale = d ** -0.5

    def block(args):
        q_i, c_i, p_i = args
        s = jnp.einsum('bhqd,bhkd->bhqk', q_i, kh).astype(jnp.float32) * scale
        s = s + c_i[..., None] - c[:, :, None, :]
        causal = p_i[:, None] >= k_pos[None, :]
        s = jnp.where(causal, s, -jnp.inf)
        p = jax.nn.softmax(s, axis=-1).astype(vh.dtype)
        return jnp.einsum('bhqk,bhkd->bhqd', p, vh)

    o = lax.map(block, (qb, cb, q_pos))
    return o.transpose(1, 0, 3, 2, 4).reshape(B, S, H * d)


def sliding_window_sink_attention(q, k, v, sinks):
    B, S, Hq, d = q.shape
    Hk = k.shape[2]
    G = Hq // Hk
    W = SWA_BLOCK
    nb = S // W
    qb = q.reshape(B, nb, W, Hk, G, d)

    def with_prev(t):
        t = t.reshape(B, nb, W, Hk, d)
        prev = jnp.pad(t, ((0, 0), (1, 0), (0, 0), (0, 0), (0, 0)))[:, :-1]
        return jnp.concatenate([prev, t], axis=2)

    kw = with_prev(k)
    vw = with_prev(v)
    s = jnp.einsum('bnqhgd,bnkhd->bnhgqk', qb, kw).astype(jnp.float32) * (d ** -0.5)
    blk = jnp.arange(nb)[:, None]
    q_abs = blk * W + jnp.arange(W)[None, :]
    k_abs = blk * W - W + jnp.arange(2 * W)[None, :]
    rel = q_abs[:, :, None] - k_abs[:, None, :]
    band = (rel >= 0) & (rel < WINDOW) & (k_abs[:, None, :] >= 0)
    s = jnp.where(band[None, :, None, None], s, -jnp.inf)
    sink = jnp.broadcast_to(sinks.astype(jnp.float32).reshape(1, 1, Hk, G, 1, 1), s.shape[:-1] + (1,))
    p = jax.nn.softmax(jnp.concatenate([s, sink], axis=-1), axis=-1)[..., :-1].astype(v.dtype)
    o = jnp.einsum('bnhgqk,bnkhd->bnqhgd', p, vw)
    return o.reshape(B, S, Hq * d)


def setup_inputs(seed: int = 0) -> dict:
    key = jax.random.key(seed)
    ks = jax.random.split(key, 24)
    f32 = jnp.float32

    def w(k, shape, fan_in):
        return jax.random.normal(k, shape, f32) * (fan_in ** -0.5)

    def gain(k, shape):
        return 1.0 + 0.02 * jax.random.normal(k, shape, f32)

    x = jax.random.normal(ks[0], (BATCH, SEQ, D_MODEL), f32)
    positions = jnp.broadcast_to(jnp.arange(SEQ, dtype=jnp.int32), (BATCH, SEQ))
    return {
        "x": x,
        "positions": positions,
        "norm_ffn1_g": gain(ks[1], (DEPTH, D_MODEL)),
        "ffn1_w_gate": w(ks[2], (DEPTH, D_MODEL, D_FF), D_MODEL),
        "ffn1_w_up": w(ks[3], (DEPTH, D_MODEL, D_FF), D_MODEL),
        "ffn1_w_down": w(ks[4], (DEPTH, D_FF, D_MODEL), D_FF),
        "norm_mix_g": gain(ks[5], (DEPTH, D_MODEL)),
        "w_in": w(ks[6], (DEPTH, D_MODEL, D_IN_PROJ), D_MODEL),
        "b_forget": 0.1 * jax.random.normal(ks[7], (DEPTH, FOX_HEADS), f32),
        "fox_q_norm_g": gain(ks[8], (DEPTH, HEAD_DIM)),
        "fox_k_norm_g": gain(ks[9], (DEPTH, HEAD_DIM)),
        "swa_q_norm_g": gain(ks[10], (DEPTH, HEAD_DIM)),
        "swa_k_norm_g": gain(ks[11], (DEPTH, HEAD_DIM)),
        "swa_sinks": 0.5 * jax.random.normal(ks[12], (DEPTH, SWA_Q_HEADS), f32),
        "out_norm_fox_g": gain(ks[13], (DEPTH, FOX_HEADS * HEAD_DIM)),
        "out_norm_swa_g": gain(ks[14], (DEPTH, SWA_Q_HEADS * HEAD_DIM)),
        "w_out": w(ks[15], (DEPTH, D_MIX, D_MODEL), D_MIX),
        "norm_ffn2_g": gain(ks[16], (DEPTH, D_MODEL)),
        "ffn2_w_gate": w(ks[17], (DEPTH, D_MODEL, D_FF), D_MODEL),
        "ffn2_w_up": w(ks[18], (DEPTH, D_MODEL, D_FF), D_MODEL),
        "ffn2_w_down": w(ks[19], (DEPTH, D_FF, D_MODEL), D_FF),
    }


def reference(x, positions, norm_ffn1_g, ffn1_w_gate, ffn1_w_up, ffn1_w_down, norm_mix_g, w_in,
              b_forget, fox_q_norm_g, fox_k_norm_g, swa_q_norm_g, swa_k_norm_g, swa_sinks,
              out_norm_fox_g, out_norm_swa_g, w_out, norm_ffn2_g, ffn2_w_gate, ffn2_w_up, ffn2_w_down):
    B, S, _ = x.shape
    h = x
    for l in range(DEPTH):
        h = h + 0.5 * swiglu(rms_norm(h, norm_ffn1_g[l]), ffn1_w_gate[l], ffn1_w_up[l], ffn1_w_down[l])

        u = rms_norm(h, norm_mix_g[l])
        proj = u @ w_in[l]
        q_f, k_f, v_f, f_logit, q_s, k_s, v_s = jnp.split(proj, SPLIT_POINTS, axis=-1)

        q_f = rms_norm(q_f.reshape(B, S, FOX_HEADS, HEAD_DIM), fox_q_norm_g[l])
        k_f = rms_norm(k_f.reshape(B, S, FOX_HEADS, HEAD_DIM), fox_k_norm_g[l])
        v_f = v_f.reshape(B, S, FOX_HEADS, HEAD_DIM)
        log_f = jax.nn.log_sigmoid((f_logit + b_forget[l]).astype(jnp.float32))
        o_fox = forgetting_attention(q_f, k_f, v_f, log_f)

        q_s = rope(rms_norm(q_s.reshape(B, S, SWA_Q_HEADS, HEAD_DIM), swa_q_norm_g[l]), positions)
        k_s = rope(rms_norm(k_s.reshape(B, S, SWA_KV_HEADS, HEAD_DIM), swa_k_norm_g[l]), positions)
        v_s = v_s.reshape(B, S, SWA_KV_HEADS, HEAD_DIM)
        o_swa = sliding_window_sink_attention(q_s, k_s, v_s, swa_sinks[l])

        o = jnp.concatenate([rms_norm(o_fox, out_norm_fox_g[l]), rms_norm(o_swa, out_norm_swa_g[l])], axis=-1)
        h = h + o @ w_out[l]

        h = h + 0.5 * swiglu(rms_norm(h, norm_ffn2_g[l]), ffn2_w_gate[l], ffn2_w_up[l], ffn2_w_down[l])
    return h
```

```python
import contextlib
import numpy as np
import concourse.bass as bass
import concourse.mybir as mybir
from concourse.bass_utils import run_bass_kernel_spmd

F32 = mybir.dt.float32
I32 = mybir.dt.int32
F32R = mybir.dt.float32r
ALU = mybir.AluOpType
AF = mybir.ActivationFunctionType

S = 2048
D = 2048
DFF = 5632
NH = 16
HD = 64
NKV = 4
DIN = 4624
TT = 256
NT = S // TT
KC = D // 128
FC = DFF // 128
EPS = 1e-6
NCORES = 8
WCOLS = 256
NR = 3
WU = 128
FPARTS = ((0, 12), (12, 12), (24, 10), (34, 10))
NEG = -30000.0

O_QF, O_KF, O_VF, O_F, O_QS, O_KS, O_VS = 0, 1024, 2048, 3072, 3088, 4112, 4368


class Eng:
    def __init__(self, nc, eng, name, es):
        self.eng = eng
        self.sem = es.enter_context(nc.semaphore(name))
        self.cnt = 0
        self.step = 1
        self.seen = {}
        self.pending = []

    def wait(self, *toks):
        for t in toks:
            if t is None:
                continue
            src, v = t
            if self.seen.get(src, 0) >= v:
                continue
            if isinstance(src, Chan):
                v = src.cnt
            self.eng.wait_ge(src.sem, v * src.step)
            self.seen[src] = v

    def done(self, ins):
        self.cnt += 1
        ins.then_inc(self.sem, 1)
        return (self, self.cnt)


class Chan:
    def __init__(self, nc, name, es):
        self.sem = es.enter_context(nc.semaphore(name))
        self.cnt = 0
        self.step = 16


class Buf:
    def __init__(self, name, multi=False):
        self.name = name
        self.multi = multi
        self.w = {}
        self.r = {}
        self.wl = []

    def rtoks(self):
        return list(self.w.values()) + self.wl

    def wtoks(self):
        if self.multi:
            return list(self.r.values())
        return list(self.w.values()) + list(self.r.values())

    def set_w(self, tok):
        if self.multi:
            self.wl.append(tok)
        else:
            self.w = {tok[0]: tok}
            self.r = {}

    def set_r(self, tok):
        self.r[tok[0]] = tok


def build(stage=3, debug=False, ktiles=NT, kstop=None):
    nc = bass.Bass("TRN2", target_bir_lowering=False)

    def din(name, shape, dt=F32):
        return nc.dram_tensor(name, shape, dt, kind="ExternalInput").ap()

    skind = "ExternalOutput" if debug else "Internal"

    def dscr(name, shape):
        return nc.dram_tensor(name, shape, F32, kind=skind).ap()

    x = din("x", [S, D])
    pos = din("pos", [1, S], I32)
    wg1 = din("wg1", [D, DFF])
    wu1 = din("wu1", [D, DFF])
    wd1 = din("wd1", [DFF, D])
    win = din("win", [D, DIN])
    wout = din("wout", [D, D])
    wg2 = din("wg2", [D, DFF])
    wu2 = din("wu2", [D, DFF])
    wd2 = din("wd2", [DFF, D])
    gcols_d = din("gcols", [128, 64])
    qkg_d = din("qkg", [128, 4])
    bfs_d = din("bfs", [128, 32])
    cmat_d = din("cmat", [128, 8 * 128])
    invf_d = din("invf", [128, 1])
    out = nc.dram_tensor("out", [S, D], F32, kind="ExternalOutput").ap()

    h1T = dscr("h1T", [D, S])
    qf = dscr("qf", [NH, 65, S])
    kf = dscr("kf", [NH, 65, S])
    vf = dscr("vf", [S, 1024])
    lf = dscr("lf", [S, 16])
    qs = dscr("qs", [NH, 64, S])
    ks = dscr("ks", [NKV, 64, S])
    vs = dscr("vs", [S, 256])
    oT = dscr("oT", [D, S])
    cs = dscr("cs", [2, 128, S])

    with contextlib.ExitStack() as es:
        pe = Eng(nc, nc.tensor, "s_pe", es)
        act = Eng(nc, nc.scalar, "s_act", es)
        dve = Eng(nc, nc.vector, "s_dve", es)
        sync = Eng(nc, nc.sync, "s_sync", es)
        pool = Eng(nc, nc.gpsimd, "s_pool", es)
        engines = [pe, act, dve, sync, pool]
        chans = []

        def chan(name):
            c = Chan(nc, name, es)
            chans.append(c)
            return c

        def sbuf(stack, name, shape, dt=F32):
            return stack.enter_context(nc.sbuf_tensor("sb_" + name, shape, dt))

        def cop(E, fn, reads=(), writes=()):
            for b in reads:
                E.wait(*b.rtoks())
            for b in writes:
                E.wait(*b.wtoks())
            tok = E.done(fn())
            for b in reads:
                b.set_r(tok)
            for b in writes:
                b.set_w(tok)
            return tok

        def dma(Q, ch, out_ap, in_ap, reads=(), writes=()):
            for b in reads:
                Q.wait(*b.rtoks())
            for b in writes:
                Q.wait(*b.wtoks())
            Q.eng.dma_start(out=out_ap, in_=in_ap).then_inc(ch.sem, 16)
            ch.cnt += 1
            tok = (ch, ch.cnt)
            for b in reads:
                b.set_r(tok)
            for b in writes:
                b.set_w(tok)
            return tok

        def mm(out_ap, lhsT, rhs, start, stop, bank, reads=(), ms=None, transpose=False,
               wait_bank=None, set_bank=None):
            if wait_bank is None:
                wait_bank = start
            if set_bank is None:
                set_bank = stop
            if ms is None:
                ms = set_bank
            for b in reads:
                pe.wait(*b.rtoks())
            if wait_bank:
                pe.wait(*bank.wtoks())
            if transpose:
                ins = nc.tensor.transpose(out_ap, lhsT, rhs)
            else:
                ins = nc.tensor.matmul(out_ap, lhsT, rhs, start=start, stop=stop)
            pe.pending.extend(reads)
            tok = None
            if ms:
                tok = pe.done(ins)
                for b in pe.pending:
                    b.set_r(tok)
                pe.pending = []
            if set_bank:
                assert tok is not None
                bank.set_w(tok)
            return tok

        def barrier():
            assert not pe.pending
            toks = [(e, e.cnt) for e in (pe, act, dve) if e.cnt > 0]
            toks += [(c, c.cnt) for c in chans if c.cnt > 0]
            for e in engines:
                e.wait(*toks)

        banks = [es.enter_context(nc.psum_tensor(f"bank{i}", [128, 512], F32)) for i in range(8)]
        bankB = [Buf(f"bank{i}") for i in range(8)]
        ring = {"i": 0, "n": 8}

        def next_bank():
            b = ring["i"] % ring["n"]
            ring["i"] += 1
            assert not getattr(bankB[b], "held", False), "PSUM bank re-allocated while a pipelined unit still holds it"
            return banks[b], bankB[b]

        cmat = sbuf(es, "cmat", [128, 8 * 128])
        gcols = sbuf(es, "gcols", [128, 64])
        qkg = sbuf(es, "qkg", [128, 4])
        bfs = sbuf(es, "bfs", [128, 32])
        invf = sbuf(es, "invf", [128, 1])
        epsc = sbuf(es, "epsc", [128, 1])
        constB = Buf("const")
        ch_c = chan("c_const")
        dma(pool, ch_c, cmat[:], cmat_d[:, :], writes=[constB])
        dma(pool, ch_c, gcols[:], gcols_d[:, :], writes=[constB])
        dma(pool, ch_c, qkg[:], qkg_d[:, :], writes=[constB])
        dma(pool, ch_c, bfs[:], bfs_d[:, :], writes=[constB])
        dma(pool, ch_c, invf[:], invf_d[:, :], writes=[constB])
        cop(dve, lambda: nc.vector.memset(epsc[:], EPS), writes=[constB])
        cop(dve, lambda: nc.vector.tensor_scalar(qkg[:, 0:1], qkg[:, 0:1], 0.125, None, ALU.mult),
            reads=[constB], writes=[constB])
        cop(dve, lambda: nc.vector.tensor_scalar(qkg[:, 2:3], qkg[:, 2:3], 0.125, None, ALU.mult),
            reads=[constB], writes=[constB])
        cop(act, lambda: nc.scalar.activation(out=bfs[:, 16:32], in_=bfs[:, 16:32], func=AF.Exp),
            reads=[constB], writes=[constB])
        ident = cmat[:, 0:128]
        ones = cmat[:, 128:256]
        blk1 = cmat[:, 256:384]
        rrot = cmat[:, 384:512]
        mcur = cmat[:, 512:640]
        mprev = cmat[:, 640:768]
        tri = cmat[:, 768:896]
        e65 = cmat[:, 896:960]
        G_FFN1, G_MIX, G_FFN2, G_OUT = 0, 16, 32, 48

        whi = [sbuf(es, f"whi{i}", [128, 16, WU]) for i in range(2)]
        wlo = [sbuf(es, f"wlo{i}", [128, 16, WU]) for i in range(2)]
        whiB = [Buf(f"whi{i}") for i in range(2)]
        wloB = [Buf(f"wlo{i}") for i in range(2)]
        uh = sbuf(es, "uh", [128, KC, TT]); uhG = [Buf(f"uh{g}") for g in range(4)]
        ul = sbuf(es, "ul", [128, KC, TT]); ulG = [Buf(f"ul{g}") for g in range(4)]
        ah = sbuf(es, "ah", [128, 12, TT]); ahG = [Buf(f"ah{j}") for j in range(12)]
        al = sbuf(es, "al", [128, 12, TT]); alG = [Buf(f"al{j}") for j in range(12)]
        wch = [chan(f"c_w{i}") for i in range(NR)]
        rawB = [Buf(f"wraw{i}") for i in range(NR)]
        rawh = {"t": None}

        def R(ap):
            return ap.bitcast(F32R)

        def mm3(out_ap, l2, r2, start, stop, bank, reads):
            lh, ll = l2
            rh, rl = r2
            mm(out_ap, R(lh), R(rh), start=start, stop=False, bank=bank, reads=reads)
            mm(out_ap, R(lh), R(rl), start=False, stop=False, bank=bank)
            mm(out_ap, R(ll), R(rh), start=False, stop=stop, bank=bank)

        class WS:
            def __init__(self, descs):
                self.descs = descs
                self.units = [(i, h) for i, d in enumerate(descs) for h in range(max(1, d[4] // WU))]
                self.n_dma = 0
                self.n_prep = 0
                self.n_get = 0
                self.lim_d = 0
                self.lim_u = 0

            def open_phase(self, n_descs):
                self.lim_d += n_descs
                self.lim_u = sum(1 for (i, h) in self.units if i < self.lim_d)

            def _dma(self, i):
                w_ap, r0, nk, c0, ncols = self.descs[i]
                s_ = i % NR
                src = w_ap[r0:r0 + nk * 128, c0:c0 + ncols].rearrange("(k p) f -> p k f", p=128)
                dma(sync, wch[s_], rawh["t"][s_][:, 0:nk, 0:ncols], src, writes=[rawB[s_]])

            def _prep(self, u):
                i, h = self.units[u]
                w_ap, r0, nk, c0, ncols = self.descs[i]
                s_ = i % NR
                hs = u % 2
                cw = min(WU, ncols)
                src = rawh["t"][s_][:, 0:nk, h * WU:h * WU + cw]
                cop(act, lambda: nc.scalar.activation(out=R(whi[hs][:, 0:nk, 0:cw]), in_=src, func=AF.Copy),
                    reads=[rawB[s_]], writes=[whiB[hs]])
                cop(dve, lambda: nc.vector.tensor_tensor(out=R(wlo[hs][:, 0:nk, 0:cw]), in0=src,
                                                         in1=whi[hs][:, 0:nk, 0:cw], op=ALU.subtract),
                    reads=[rawB[s_], whiB[hs]], writes=[wloB[hs]])

            def get(self, desc):
                u = self.n_get
                i, h = self.units[u]
                assert self.descs[i][1:] == desc[1:] and self.descs[i][0] is desc[0], (u, i, desc[1:], self.descs[i][1:])
                while self.n_dma < min(self.lim_d, i + NR):
                    self._dma(self.n_dma)
                    self.n_dma += 1
                while self.n_prep < min(self.lim_u, u + 2):
                    self._prep(self.n_prep)
                    self.n_prep += 1
                self.n_get += 1
                hs = u % 2
                return whi[hs], wlo[hs], [whiB[hs], wloB[hs]]

        def ffn_descs(wg, wu, wd):
            out_ = []
            for (f0, nf) in FPARTS:
                for fp in range(nf // 2):
                    c0 = (f0 + 2 * fp) * 128
                    out_.append((wg, 0, KC, c0, 256))
                    out_.append((wu, 0, KC, c0, 256))
                for dp in range(D // 256):
                    out_.append((wd, f0 * 128, nf, dp * 256, 256))
            return out_

        def inproj_descs():
            out_ = []
            for base, n in ((O_QF, 4), (O_KF, 4), (O_VF, 4)):
                out_ += [(win, 0, KC, base + q4 * 256, 256) for q4 in range(n)]
            out_.append((win, 0, KC, O_F, 16))
            out_ += [(win, 0, KC, O_QS + q4 * 256, 256) for q4 in range(4)]
            out_.append((win, 0, KC, O_KS, 256))
            out_.append((win, 0, KC, O_VS, 256))
            return out_

        descsA = []
        descsC = []
        if stage >= 1:
            for t in range(NT):
                descsA += ffn_descs(wg1, wu1, wd1) + inproj_descs()
        if stage >= 3:
            for t in range(NT):
                descsC += [(wout, 0, KC, dp * 256, 256) for dp in range(D // 256)] + ffn_descs(wg2, wu2, wd2)
        ws = WS(descsA + descsC)

        csB = Buf("cs", multi=True)
        with contextlib.ExitStack() as ps:
            posi = sbuf(ps, "posi", [128, S], I32)
            ang = sbuf(ps, "ang", [128, S])
            tt_ = sbuf(ps, "tt_", [128, S])
            ki = sbuf(ps, "ki", [128, S], I32)
            kfl = sbuf(ps, "kfl", [128, S])
            res = sbuf(ps, "res", [128, S])
            pB, aB, tB, kB_, kfB, rB = (Buf(n) for n in ("posi", "ang", "tt", "ki", "kfl", "res"))
            ch_p = chan("c_pro")
            dma(pool, ch_p, posi[:], pos[0, :].partition_broadcast(128), writes=[pB])
            cop(dve, lambda: nc.vector.tensor_copy(ang[:], posi[:]), reads=[pB], writes=[aB])
            cop(dve, lambda: nc.vector.tensor_scalar(ang[:], ang[:], invf[:, 0:1], 1.0 / (2.0 * np.pi), ALU.mult, ALU.mult),
                reads=[aB, constB], writes=[aB])
            for which, shift in ((1, 0.0), (0, 0.25)):
                cop(dve, lambda: nc.vector.tensor_scalar(tt_[:], ang[:], shift, None, ALU.add), reads=[aB], writes=[tB])
                cop(dve, lambda: nc.vector.tensor_copy(ki[:], tt_[:]), reads=[tB], writes=[kB_])
                cop(dve, lambda: nc.vector.tensor_copy(kfl[:], ki[:]), reads=[kB_], writes=[kfB])
                cop(dve, lambda: nc.vector.tensor_tensor(out=tt_[:], in0=tt_[:], in1=kfl[:], op=ALU.subtract),
                    reads=[tB, kfB], writes=[tB])
                cop(act, lambda: nc.scalar.activation(out=res[:], in_=tt_[:], func=AF.Sin, scale=6.283185),
                    reads=[tB], writes=[rB])
                dma(pool, ch_p, cs[which, :, :], res[:], reads=[rB], writes=[csB])
            barrier()

        def rmsnorm3(xT, xB, sq, sqB, sd, sdB, tmpu, tmpuB, gofs, c_lo, c_hi, dim):
            n = c_hi - c_lo
            cop(act, lambda: nc.scalar.activation(out=sq[:, 0:n, :], in_=xT[:, c_lo:c_hi, :], func=AF.Square),
                reads=[xB], writes=[sqB])
            cop(dve, lambda: nc.vector.tensor_reduce(out=sd[:], in_=sq[:, 0:n, :].rearrange("p c t -> p t c"),
                                                     axis=mybir.AxisListType.X, op=ALU.add),
                reads=[sqB], writes=[sdB])
            bk, bB = next_bank()
            mm(bk[:, 0:TT], ones, sd[:], start=True, stop=True, bank=bB, reads=[sdB, constB])
            cop(act, lambda: nc.scalar.activation(out=sd[:], in_=bk[:, 0:TT], func=AF.Sqrt, bias=epsc[:, 0:1], scale=1.0 / dim),
                reads=[bB, constB], writes=[sdB])
            cop(dve, lambda: nc.vector.reciprocal(sd[:], sd[:]), reads=[sdB], writes=[sdB])
            for g in range(n // 4):
                tb, tbB = tmpu[g % 2], tmpuB[g % 2]
                c0 = c_lo + 4 * g
                for c4 in range(4):
                    c = c0 + c4
                    cop(dve, lambda: nc.vector.scalar_tensor_tensor(
                        out=tb[:, c4, :], in0=xT[:, c, :], scalar=gcols[:, gofs + c:gofs + c + 1], in1=sd[:],
                        op0=ALU.mult, op1=ALU.mult), reads=[xB, sdB, constB], writes=[tbB])
                cop(act, lambda: nc.scalar.activation(out=R(uh[:, c0:c0 + 4, :]), in_=tb[:], func=AF.Copy),
                    reads=[tbB], writes=[uhG[c0 // 4]])
                cop(dve, lambda: nc.vector.tensor_tensor(out=R(ul[:, c0:c0 + 4, :]), in0=tb[:], in1=uh[:, c0:c0 + 4, :],
                                                         op=ALU.subtract), reads=[tbB, uhG[c0 // 4]], writes=[ulG[c0 // 4]])

        def ffn3(xT, xB, sgs, sgB, tmpa, tmpaB, wg, wu, wd):
            for (f0, nf) in FPARTS:
                for fp in range(nf // 2):
                    c0 = (f0 + 2 * fp) * 128
                    gb = []
                    for wmat in (wg, wu):
                        for half in range(2):
                            wh_, wl_, wB_ = ws.get((wmat, 0, KC, c0, 256))
                            bk, bB = next_bank()
                            for k in range(KC):
                                mm3(bk[:, 0:TT], (wh_[:, k, :], wl_[:, k, :]), (uh[:, k, :], ul[:, k, :]),
                                    start=(k == 0), stop=(k == KC - 1), bank=bB, reads=wB_ + [uhG[k // 4], ulG[k // 4]])
                            gb.append((bk, bB))
                    for half in range(2):
                        j = 2 * fp + half
                        g_, gB_ = gb[half]
                        u_, uB_ = gb[2 + half]
                        sg, sB = sgs[j % 2], sgB[j % 2]
                        ta, taB = tmpa[j % 2], tmpaB[j % 2]
                        cop(act, lambda: nc.scalar.activation(out=sg[:], in_=g_[:, 0:TT], func=AF.Silu),
                            reads=[gB_], writes=[sB])
                        cop(dve, lambda: nc.vector.tensor_tensor(out=ta[:], in0=sg[:], in1=u_[:, 0:TT], op=ALU.mult),
                            reads=[sB, uB_], writes=[taB])
                        cop(act, lambda: nc.scalar.activation(out=R(ah[:, j, :]), in_=ta[:], func=AF.Copy),
                            reads=[taB], writes=[ahG[j]])
                        cop(dve, lambda: nc.vector.tensor_tensor(out=R(al[:, j, :]), in0=ta[:], in1=ah[:, j, :], op=ALU.subtract),
                            reads=[taB, ahG[j]], writes=[alG[j]])
                for dp in range(D // 256):
                    for half in range(2):
                        dm = 2 * dp + half
                        wh_, wl_, wB_ = ws.get((wd, f0 * 128, nf, dp * 256, 256))
                        bk, bB = next_bank()
                        for j in range(nf):
                            mm3(bk[:, 0:TT], (wh_[:, j, :], wl_[:, j, :]), (ah[:, j, :], al[:, j, :]),
                                start=(j == 0), stop=(j == nf - 1), bank=bB, reads=wB_ + [ahG[j], alG[j]])
                        cop(dve, lambda: nc.vector.scalar_tensor_tensor(
                            out=xT[:, dm, :], in0=bk[:, 0:TT], scalar=0.5, in1=xT[:, dm, :],
                            op0=ALU.mult, op1=ALU.add), reads=[bB, xB], writes=[xB])

        h1B = Buf("h1T", multi=True)
        qfB, kfB2, vfB, lfB, qsB, ksB, vsB, oTB = (Buf(n, multi=True) for n in
                                                   ("qf", "kf", "vf", "lf", "qs", "ks", "vs", "oT"))

        with contextlib.ExitStack() as pa:
            raw = [sbuf(pa, f"wraw{i}", [128, 16, 256]) for i in range(NR)]
            rawh["t"] = raw
            ws.open_phase(len(descsA))
            xT = sbuf(pa, "xT", [128, KC, TT]); xB = Buf("xT")
            arena = sbuf(pa, "arena", [128, KC * TT]); arB = Buf("arena")
            xin = [arena[:, 0:D], arena[:, D:2 * D]]
            sq = arena[:, :].rearrange("p (c t) -> p c t", c=KC)
            tmpu = [sbuf(pa, f"tmpu{i}", [128, 4, TT]) for i in range(2)]
            tmpuB = [Buf(f"tmpu{i}") for i in range(2)]
            tmpa = [sbuf(pa, f"tmpa{i}", [128, TT]) for i in range(2)]
            tmpaB = [Buf(f"tmpa{i}") for i in range(2)]
            sd = sbuf(pa, "sd", [128, TT]); sdB = Buf("sd")
            sgs = [sbuf(pa, f"sg{i}", [128, TT]) for i in range(2)]
            sgB = [Buf(f"sg{i}") for i in range(2)]
            cst = sbuf(pa, "cst", [128, 2, TT]); cstB = Buf("cst")
            qn = [sbuf(pa, f"qn{i}", [128, TT]) for i in range(2)]
            qnB = [Buf(f"qn{i}") for i in range(2)]
            qo = [sbuf(pa, f"qo{i}", [128, TT]) for i in range(2)]
            qoB = [Buf(f"qo{i}") for i in range(2)]
            t1 = sbuf(pa, "t1", [128, TT]); t1B = Buf("t1")
            sq2 = sbuf(pa, "sq2", [128, TT]); sq2B = Buf("sq2")
            sd2 = sbuf(pa, "sd2", [128, TT]); sd2B = Buf("sd2")
            vT = [sbuf(pa, f"vT{i}", [128, TT]) for i in range(2)]
            vTB = [Buf(f"vT{i}") for i in range(2)]
            vst = [sbuf(pa, f"vst{i}", [128, 2, 128]) for i in range(2)]
            vstB = [Buf(f"vst{i}") for i in range(2)]
            fT = sbuf(pa, "fT", [16, TT]); fTB = Buf("fT")
            lft = [sbuf(pa, f"lft{i}", [128, 16]) for i in range(2)]
            lftB = [Buf(f"lft{i}") for i in range(2)]
            ch_x = chan("c_xin")
            ch_xa = chan("c_xin_act")
            ch_h = chan("c_h1")
            ch_cs = chan("c_cs")
            ch_q = [chan(f"c_qo{i}") for i in range(2)]
            ch_v = [chan(f"c_v{i}") for i in range(2)]
            ch_l = chan("c_lf")
            tog = {"q": 0, "v": 0}

            n_tiles = min(NT, ktiles) if stage >= 1 else 0
            for t in range(n_tiles):
                t0 = t * TT
                if t == 0:
                    for i in range(2):
                        dma(pool, ch_x, xin[i], x[t0 + i * 128:t0 + (i + 1) * 128, :], writes=[arB])
                for i in range(2):
                    for g in range(4):
                        bk, bB = next_bank()
                        for c4 in range(4):
                            c = 4 * g + c4
                            mm(bk[:, c4 * 128:(c4 + 1) * 128], xin[i][:, c * 128:(c + 1) * 128], ident,
                               start=True, stop=True, bank=bB, reads=[arB, constB], transpose=True,
                               wait_bank=(c4 == 0), set_bank=(c4 == 3))
                        cop(dve, lambda: nc.vector.tensor_copy(
                            xT[:, 4 * g:4 * g + 4, i * 128:(i + 1) * 128],
                            bk[:, :].rearrange("p (c n) -> p c n", c=4)), reads=[bB], writes=[xB])
                rmsnorm3(xT, xB, sq, arB, sd, sdB, tmpu, tmpuB, G_FFN1, 0, KC, D)
                if kstop == 'norm':
                    break
                ffn3(xT, xB, sgs, sgB, tmpa, tmpaB, wg1, wu1, wd1)
                if kstop == 'ffn':
                    break
                dma(pool, ch_h, h1T[:, t0:t0 + TT].rearrange("(c p) t -> p c t", p=128), xT[:], reads=[xB], writes=[h1B])
                rmsnorm3(xT, xB, sq, arB, sd, sdB, tmpu, tmpuB, G_MIX, 0, KC, D)
                dma(pool, ch_cs, cst[:], cs[:, :, t0:t0 + TT].rearrange("a p t -> p a t"), reads=[csB], writes=[cstB])

                def proj_unit(desc):
                    wh_, wl_, wB_ = ws.get(desc)
                    m = min(128, desc[4])
                    pb, pB2 = next_bank()
                    for k in range(KC):
                        mm3(pb[0:m, 0:TT], (wh_[:, k, 0:m], wl_[:, k, 0:m]), (uh[:, k, :], ul[:, k, :]),
                            start=(k == 0), stop=(k == KC - 1), bank=pB2, reads=wB_ + [uhG[k // 4], ulG[k // 4]])
                    return pb, pB2

                def qk_unit(c0, cc, gcol, rope, dst, dstB, h0):
                    pb, pB2 = proj_unit((win, 0, KC, c0, 256))
                    pB2.held = True
                    yield
                    cop(act, lambda: nc.scalar.activation(out=sq2[:], in_=pb[:, 0:TT], func=AF.Square),
                        reads=[pB2], writes=[sq2B])
                    sb_, sB2 = next_bank()
                    sB2.held = True
                    mm(sb_[:, 0:TT], blk1, sq2[:], start=True, stop=True, bank=sB2, reads=[sq2B, constB])
                    yield
                    cop(act, lambda: nc.scalar.activation(out=sd2[:], in_=sb_[:, 0:TT], func=AF.Sqrt,
                                                          bias=epsc[:, 0:1], scale=1.0 / HD),
                        reads=[sB2, constB], writes=[sd2B])
                    sB2.held = False
                    cop(dve, lambda: nc.vector.reciprocal(sd2[:], sd2[:]), reads=[sd2B], writes=[sd2B])
                    tog["q"] += 1
                    ix = tog["q"] % 2
                    o_, oB_, och = qo[ix], qoB[ix], ch_q[ix]
                    if not rope:
                        cop(dve, lambda: nc.vector.scalar_tensor_tensor(
                            out=o_[:], in0=pb[:, 0:TT], scalar=qkg[:, gcol:gcol + 1], in1=sd2[:],
                            op0=ALU.mult, op1=ALU.mult), reads=[pB2, sd2B, constB], writes=[oB_])
                        pB2.held = False
                    else:
                        n_, nB_ = qn[ix], qnB[ix]
                        cop(dve, lambda: nc.vector.scalar_tensor_tensor(
                            out=n_[:], in0=pb[:, 0:TT], scalar=qkg[:, gcol:gcol + 1], in1=sd2[:],
                            op0=ALU.mult, op1=ALU.mult), reads=[pB2, sd2B, constB], writes=[nB_])
                        pB2.held = False
                        rb, rB2 = next_bank()
                        rB2.held = True
                        mm(rb[:, 0:TT], rrot, n_[:], start=True, stop=True, bank=rB2, reads=[nB_, constB])
                        yield
                        cop(dve, lambda: nc.vector.tensor_tensor(out=t1[:], in0=rb[:, 0:TT], in1=cst[:, 1, :], op=ALU.mult),
                            reads=[rB2, cstB], writes=[t1B])
                        rB2.held = False
                        cop(dve, lambda: nc.vector.tensor_tensor(out=o_[:], in0=n_[:], in1=cst[:, 0, :], op=ALU.mult),
                            reads=[nB_, cstB], writes=[oB_])
                        cop(dve, lambda: nc.vector.tensor_tensor(out=o_[:], in0=o_[:], in1=t1[:], op=ALU.add),
                            reads=[oB_, t1B], writes=[oB_])
                    for hh in range(2):
                        h = h0 + 2 * cc + hh
                        dma(pool, och, dst[h, 0:64, t0:t0 + TT], o_[hh * 64:(hh + 1) * 64, :], reads=[oB_], writes=[dstB])

                def v_unit(c0, cc, dst, dstB, d0):
                    pb, pB2 = proj_unit((win, 0, KC, c0, 256))
                    pB2.held = True
                    yield
                    tog["v"] += 1
                    ix = tog["v"] % 2
                    cop(act, lambda: nc.scalar.copy(out=vT[ix][:], in_=pb[:, 0:TT]), reads=[pB2], writes=[vTB[ix]])
                    pB2.held = False
                    yield
                    tb_, tB_ = next_bank()
                    tB_.held = True
                    for i in range(2):
                        mm(tb_[:, i * 128:(i + 1) * 128], vT[ix][:, i * 128:(i + 1) * 128], ident,
                           start=True, stop=True, bank=tB_, reads=[vTB[ix], constB], transpose=True,
                           wait_bank=(i == 0), set_bank=(i == 1))
                    yield
                    cop(dve, lambda: nc.vector.tensor_copy(vst[ix][:], tb_[:, 0:256].rearrange("p (i n) -> p i n", i=2)),
                        reads=[tB_], writes=[vstB[ix]])
                    tB_.held = False
                    for i in range(2):
                        dma(pool, ch_v[ix], dst[t0 + i * 128:t0 + (i + 1) * 128, d0 + cc * 128:d0 + (cc + 1) * 128],
                            vst[ix][:, i, :], reads=[vstB[ix]], writes=[dstB])

                def f_unit():
                    pb, pB2 = proj_unit((win, 0, KC, O_F, 16))
                    pB2.held = True
                    yield
                    cop(act, lambda: nc.scalar.copy(out=fT[0:16, :], in_=pb[0:16, 0:TT]), reads=[pB2], writes=[fTB])
                    pB2.held = False
                    yield
                    tbs = []
                    for i in range(2):
                        tb_, tB_ = next_bank()
                        tB_.held = True
                        mm(tb_[:, 0:16], fT[0:16, i * 128:(i + 1) * 128], cmat[0:16, 0:16], start=True, stop=True, bank=tB_,
                           reads=[fTB, constB])
                        tbs.append((tb_, tB_))
                    yield
                    for i in range(2):
                        tb_, tB_ = tbs[i]
                        cop(dve, lambda: nc.vector.tensor_tensor(out=lft[i][:], in0=tb_[:, 0:16], in1=bfs[:, 0:16], op=ALU.add),
                            reads=[tB_, constB], writes=[lftB[i]])
                        tB_.held = False
                        cop(act, lambda: nc.scalar.activation(out=lft[i][:], in_=lft[i][:], func=AF.Sigmoid),
                            reads=[lftB[i]], writes=[lftB[i]])
                        cop(act, lambda: nc.scalar.activation(out=lft[i][:], in_=lft[i][:], func=AF.Ln),
                            reads=[lftB[i]], writes=[lftB[i]])
                        dma(pool, ch_l, lf[t0 + i * 128:t0 + (i + 1) * 128, :], lft[i][:], reads=[lftB[i]], writes=[lfB])

                units = []
                for q4 in range(4):
                    units += [qk_unit(O_QF + q4 * 256, cc, 0, False, qf, qfB, 4 * q4) for cc in range(2)]
                for q4 in range(4):
                    units += [qk_unit(O_KF + q4 * 256, cc, 1, False, kf, kfB2, 4 * q4) for cc in range(2)]
                for q4 in range(4):
                    units += [v_unit(O_VF + q4 * 256, cc, vf, vfB, q4 * 256) for cc in range(2)]
                units.append(f_unit())
                for q4 in range(4):
                    units += [qk_unit(O_QS + q4 * 256, cc, 2, True, qs, qsB, 4 * q4) for cc in range(2)]
                units += [qk_unit(O_KS, cc, 3, True, ks, ksB, 0) for cc in range(2)]
                units += [v_unit(O_VS, cc, vs, vsB, 0) for cc in range(2)]
                active = []
                for g_ in units:
                    next(g_)
                    for h_ in list(active):
                        try:
                            next(h_)
                        except StopIteration:
                            active.remove(h_)
                    active.append(g_)
                while active:
                    for h_ in list(active):
                        try:
                            next(h_)
                        except StopIteration:
                            active.remove(h_)
                if t + 1 < n_tiles:
                    for i in range(2):
                        dma(act, ch_xa, xin[i], x[t0 + TT + i * 128:t0 + TT + (i + 1) * 128, :], writes=[arB])
            barrier()

        if stage >= 2:
            ring["n"] = 6
            ring["i"] = 0
            with contextlib.ExitStack() as pb_:
                lfs = sbuf(pb_, "lfs", [128, 16, 16]); lfsB = Buf("lfs")
                csb = sbuf(pb_, "csb", [128, 16, 16]); csbB = Buf("csb")
                negc = sbuf(pb_, "negc", [128, 16, 16]); negcB = Buf("negc")
                cT = sbuf(pb_, "cT", [16, S]); cTB = Buf("cT")
                onesrow = sbuf(pb_, "onesrow", [1, S]); orB = Buf("onesrow")
                qa = [sbuf(pb_, f"qa{i}", [65, S]) for i in range(2)]
                qaB = [Buf(f"qa{i}") for i in range(2)]
                ka = [sbuf(pb_, f"ka{i}", [65, S]) for i in range(2)]
                kaB = [Buf(f"ka{i}") for i in range(2)]
                va = [sbuf(pb_, f"va{i}", [128, 16, 65]) for i in range(2)]
                vaB = [Buf(f"va{i}") for i in range(2)]
                pt = [sbuf(pb_, f"pt{i}", [128, 512]) for i in range(3)]
                ptB = [Buf(f"pt{i}") for i in range(3)]
                accs = [sbuf(pb_, f"accs{i}", [65, 512]) for i in range(2)]
                accsB = [Buf(f"accs{i}") for i in range(2)]
                rden = sbuf(pb_, "rden", [64, 512]); rdenB = Buf("rden")
                ost = [sbuf(pb_, f"ost{i}", [64, 512]) for i in range(2)]
                ostB = [Buf(f"ost{i}") for i in range(2)]
                ch_b = chan("c_bpro")
                ch_qa = [chan(f"c_qa{i}") for i in range(2)]
                ch_ka = [chan(f"c_ka{i}") for i in range(2)]
                ch_va = [chan(f"c_va{i}") for i in range(2)]
                ch_o = [chan(f"c_ost{i}") for i in range(2)]

                dma(pool, ch_b, lfs[:], lf[:, :].rearrange("(j p) h -> p j h", p=128), reads=[lfB], writes=[lfsB])
                cop(dve, lambda: nc.vector.memset(onesrow[:], 1.0), writes=[orB])
                for i in range(2):
                    cop(dve, lambda: nc.vector.memset(va[i][:, :, 64:65], 1.0), writes=[vaB[i]])
                for j in range(16):
                    bk, bB = next_bank()
                    for i in range(j):
                        mm(bk[:, 0:16], ones, lfs[:, i, :], start=(i == 0), stop=False, bank=bB, reads=[lfsB, constB])
                    mm(bk[:, 0:16], tri, lfs[:, j, :], start=(j == 0), stop=True, bank=bB, reads=[lfsB, constB])
                    cop(dve, lambda: nc.vector.tensor_copy(csb[:, j, :], bk[:, 0:16]), reads=[bB], writes=[csbB])
                    cop(dve, lambda: nc.vector.tensor_scalar(negc[:, j, :], bk[:, 0:16], -1.0, None, ALU.mult),
                        reads=[bB], writes=[negcB])
                    b2, b2B = next_bank()
                    mm(b2[0:16, 0:128], csb[:, j, :], ident, start=True, stop=True, bank=b2B, reads=[csbB, constB])
                    cop(dve, lambda: nc.vector.tensor_copy(cT[0:16, j * 128:(j + 1) * 128], b2[0:16, 0:128]),
                        reads=[b2B], writes=[cTB])
                for h in range(NH):
                    dma(pool, ch_b, qf[h, 64:65, :], cT[h:h + 1, :], reads=[cTB], writes=[qfB])
                    dma(pool, ch_b, kf[h, 64:65, :], onesrow[0:1, :], reads=[orB], writes=[kfB2])

                accbank = [(banks[6], bankB[6]), (banks[7], bankB[7])]
                cnt = {"acc": 0, "pt": 0}
                deferred = []

                def finalize(h_row, qc, acc, accB_, sink_col, now):
                    ix = cnt["acc"] % 2
                    cnt["acc"] += 1
                    a_, aB_ = accs[ix], accsB[ix]
                    cop(dve, lambda: nc.vector.tensor_copy(a_[0:65, :], acc[0:65, :]), reads=[accB_], writes=[aB_])

                    def tail():
                        db, dB_ = next_bank()
                        mm(db[0:64, 0:512], e65[0:65, :], a_[0:65, :], start=True, stop=True, bank=dB_, reads=[aB_, constB])
                        if sink_col is None:
                            cop(dve, lambda: nc.vector.reciprocal(rden[0:64, :], db[0:64, :]), reads=[dB_], writes=[rdenB])
                        else:
                            cop(dve, lambda: nc.vector.tensor_scalar(rden[0:64, :], db[0:64, :],
                                                                     bfs[0:64, 16 + sink_col:17 + sink_col], None, ALU.add),
                                reads=[dB_, constB], writes=[rdenB])
                            cop(dve, lambda: nc.vector.reciprocal(rden[0:64, :], rden[0:64, :]), reads=[rdenB], writes=[rdenB])
                        o_, oB_ = ost[ix], ostB[ix]
                        cop(dve, lambda: nc.vector.tensor_tensor(out=o_[0:64, :], in0=a_[0:64, :], in1=rden[0:64, :], op=ALU.mult),
                            reads=[aB_, rdenB], writes=[oB_])
                        dma(pool, ch_o[ix], oT[h_row:h_row + 64, qc * 512:(qc + 1) * 512], o_[0:64, :], reads=[oB_], writes=[oTB])
                    deferred.append((now + 2, tail))

                def load_head(kind, h):
                    ib = h % 2
                    if kind == "fox":
                        dma(pool, ch_qa[ib], qa[ib][0:65, :], qf[h, :, :], reads=[qfB], writes=[qaB[ib]])
                        dma(pool, ch_ka[ib], ka[ib][0:65, :], kf[h, :, :], reads=[kfB2], writes=[kaB[ib]])
                        dma(pool, ch_va[ib], va[ib][:, :, 0:64],
                            vf[:, h * 64:(h + 1) * 64].rearrange("(j p) d -> p j d", p=128), reads=[vfB], writes=[vaB[ib]])
                    else:
                        kv = h // 4
                        dma(pool, ch_qa[ib], qa[ib][0:64, :], qs[h, :, :], reads=[qsB], writes=[qaB[ib]])
                        dma(pool, ch_ka[ib], ka[ib][0:64, :], ks[kv, :, :], reads=[ksB], writes=[kaB[ib]])
                        dma(pool, ch_va[ib], va[ib][:, :, 0:64],
                            vs[:, kv * 64:(kv + 1) * 64].rearrange("(j p) d -> p j d", p=128), reads=[vsB], writes=[vaB[ib]])

                heads = [("fox", h) for h in range(NH)] + [("swa", h) for h in range(NH)]

                def fox_step(h, qc, j, prefetch):
                    ib = h % 2
                    nj = 4 * qc + 4
                    q0 = max(512 * qc, 128 * j)
                    n = 512 * (qc + 1) - q0
                    off = q0 - 512 * qc
                    diag = j >= 4 * qc
                    st = {}

                    def s_part():
                        if prefetch is not None:
                            load_head(*prefetch)
                        sbk, sB_ = next_bank()
                        st["s"] = (sbk, sB_)
                        mm(sbk[:, 0:n], ka[ib][0:65, j * 128:(j + 1) * 128], qa[ib][0:65, q0:q0 + n],
                           start=True, stop=True, bank=sB_, reads=[kaB[ib], qaB[ib]])

                    def r_part(now, acc_holder):
                        sbk, sB_ = st["s"]
                        acc, accB_ = acc_holder["acc"]
                        ip = cnt["pt"] % 3
                        cnt["pt"] += 1
                        p_, pB_ = pt[ip], ptB[ip]
                        if diag:
                            cop(dve, lambda: nc.vector.tensor_tensor(out=p_[:, 0:128], in0=sbk[:, 0:128], in1=mcur, op=ALU.add),
                                reads=[sB_, constB], writes=[pB_])
                            cop(act, lambda: nc.scalar.activation(out=p_[:, 0:128], in_=p_[:, 0:128], func=AF.Exp,
                                                                  bias=negc[:, j, h:h + 1], scale=1.0),
                                reads=[pB_, negcB], writes=[pB_])
                            if n > 128:
                                cop(act, lambda: nc.scalar.activation(out=p_[:, 128:n], in_=sbk[:, 128:n], func=AF.Exp,
                                                                      bias=negc[:, j, h:h + 1], scale=1.0),
                                    reads=[sB_, negcB], writes=[pB_])
                        else:
                            cop(act, lambda: nc.scalar.activation(out=p_[:, 0:n], in_=sbk[:, 0:n], func=AF.Exp,
                                                                  bias=negc[:, j, h:h + 1], scale=1.0),
                                reads=[sB_, negcB], writes=[pB_])
                        mm(acc[0:65, off:512], va[ib][:, j, 0:65], p_[:, 0:n], start=(j == 0), stop=(j == nj - 1),
                           bank=accB_, reads=[vaB[ib], pB_], ms=True)
                        if j == nj - 1:
                            finalize(h * 64, qc, acc, accB_, None, now)
                    return s_part, r_part, (j == 0)

                def swa_step(h, qc, qb4, prefetch):
                    ib = h % 2
                    qb = 4 * qc + qb4
                    kbs = [kb for kb in (qb, qb - 1) if kb >= 0]
                    nn = 128 * len(kbs)
                    st = {}

                    def s_part():
                        if prefetch is not None:
                            load_head(*prefetch)
                        sbk, sB_ = next_bank()
                        st["s"] = (sbk, sB_)
                        for i, kb in enumerate(kbs):
                            mm(sbk[:, i * 128:(i + 1) * 128], ka[ib][0:64, kb * 128:(kb + 1) * 128],
                               qa[ib][0:64, qb * 128:(qb + 1) * 128], start=True, stop=True, bank=sB_,
                               reads=[kaB[ib], qaB[ib]], wait_bank=(i == 0), set_bank=(i == len(kbs) - 1))

                    def r_part(now, acc_holder):
                        sbk, sB_ = st["s"]
                        acc, accB_ = acc_holder["acc"]
                        ip = cnt["pt"] % 3
                        cnt["pt"] += 1
                        p_, pB_ = pt[ip], ptB[ip]
                        cop(dve, lambda: nc.vector.tensor_tensor(out=p_[:, 0:nn], in0=sbk[:, 0:nn], in1=cmat[:, 512:512 + nn],
                                                                 op=ALU.add), reads=[sB_, constB], writes=[pB_])
                        cop(act, lambda: nc.scalar.activation(out=p_[:, 0:nn], in_=p_[:, 0:nn], func=AF.Exp),
                            reads=[pB_], writes=[pB_])
                        for i, kb in enumerate(kbs):
                            first = (qb4 == 0 and i == 0)
                            last = (qb4 == 3 and i == len(kbs) - 1)
                            mm(acc[0:65, qb4 * 128:(qb4 + 1) * 128], va[ib][:, kb, 0:65], p_[:, i * 128:(i + 1) * 128],
                               start=(i == 0), stop=(i == len(kbs) - 1), bank=accB_, reads=[vaB[ib], pB_],
                               ms=True, wait_bank=first, set_bank=last)
                        if qb4 == 3:
                            finalize(1024 + h * 64, qc, acc, accB_, h, now)
                    return s_part, r_part, (qb4 == 0)

                steps = []
                for hi, (kind, h) in enumerate(heads):
                    nxt = heads[hi + 1] if hi + 1 < len(heads) else None
                    for qc in range(4):
                        inner = range(4 * qc + 4) if kind == "fox" else range(4)
                        for jj in inner:
                            pf = nxt if (qc == 1 and jj == 0) else None
                            steps.append((fox_step if kind == "fox" else swa_step)(h, qc, jj, pf))
                load_head(*heads[0])
                LA = 2
                acc_holder = {}
                for k in range(min(LA, len(steps))):
                    steps[k][0]()
                for i, (s_part, r_part, newacc) in enumerate(steps):
                    if newacc:
                        acc_holder["acc"] = accbank[cnt["acc"] % 2]
                    while deferred and deferred[0][0] <= i:
                        deferred.pop(0)[1]()
                    if i + LA < len(steps):
                        steps[i + LA][0]()
                    r_part(i, acc_holder)
                while deferred:
                    deferred.pop(0)[1]()
                barrier()
            ring["n"] = 8
            ring["i"] = 0

        if stage >= 3:
            with contextlib.ExitStack() as pc:
                raw = [sbuf(pc, f"wrawc{i}", [128, 16, 256]) for i in range(NR)]
                rawh["t"] = raw
                ws.open_phase(len(descsC))
                xT = sbuf(pc, "xTc", [128, KC, TT]); xB = Buf("xTc")
                oin = sbuf(pc, "oin", [128, KC, TT]); oinB = Buf("oin")
                arena = sbuf(pc, "arenac", [128, KC * TT]); arB = Buf("arenac")
                yo = [arena[:, 0:D], arena[:, D:2 * D]]
                sq = arena[:, :].rearrange("p (c t) -> p c t", c=KC)
                tmpu = [sbuf(pc, f"tmpuc{i}", [128, 4, TT]) for i in range(2)]
                tmpuB = [Buf(f"tmpuc{i}") for i in range(2)]
                tmpa = [sbuf(pc, f"tmpac{i}", [128, TT]) for i in range(2)]
                tmpaB = [Buf(f"tmpac{i}") for i in range(2)]
                sd = sbuf(pc, "sdc", [128, TT]); sdB = Buf("sdc")
                sgs = [sbuf(pc, f"sgc{i}", [128, TT]) for i in range(2)]
                sgB = [Buf(f"sgc{i}") for i in range(2)]
                ch_li = chan("c_cin")
                ch_oi = chan("c_oin")
                ch_oia = chan("c_oin_act")
                ch_y = chan("c_yo")
                outB = Buf("out", multi=True)
                for t in range(NT):
                    t0 = t * TT
                    if t == 0:
                        dma(pool, ch_oi, oin[:], oT[:, t0:t0 + TT].rearrange("(c p) t -> p c t", p=128), reads=[oTB], writes=[oinB])
                    dma(pool, ch_li, xT[:], h1T[:, t0:t0 + TT].rearrange("(c p) t -> p c t", p=128), reads=[h1B], writes=[xB])
                    rmsnorm3(oin, oinB, sq, arB, sd, sdB, tmpu, tmpuB, G_OUT, 0, 8, 1024)
                    rmsnorm3(oin, oinB, sq, arB, sd, sdB, tmpu, tmpuB, G_OUT, 8, 16, 1024)
                    if t + 1 < NT:
                        dma(act, ch_oia, oin[:], oT[:, t0 + TT:t0 + 2 * TT].rearrange("(c p) t -> p c t", p=128),
                            reads=[oTB], writes=[oinB])
                    for dp in range(D // 256):
                        for half in range(2):
                            dm = 2 * dp + half
                            wh_, wl_, wB_ = ws.get((wout, 0, KC, dp * 256, 256))
                            pb, pB2 = next_bank()
                            for k in range(KC):
                                mm3(pb[:, 0:TT], (wh_[:, k, :], wl_[:, k, :]), (uh[:, k, :], ul[:, k, :]),
                                    start=(k == 0), stop=(k == KC - 1), bank=pB2, reads=wB_ + [uhG[k // 4], ulG[k // 4]])
                            cop(dve, lambda: nc.vector.tensor_tensor(out=xT[:, dm, :], in0=pb[:, 0:TT], in1=xT[:, dm, :], op=ALU.add),
                                reads=[pB2, xB], writes=[xB])
                    rmsnorm3(xT, xB, sq, arB, sd, sdB, tmpu, tmpuB, G_FFN2, 0, KC, D)
                    ffn3(xT, xB, sgs, sgB, tmpa, tmpaB, wg2, wu2, wd2)
                    for i in range(2):
                        for g in range(4):
                            bk, bB = next_bank()
                            for c4 in range(4):
                                c = 4 * g + c4
                                mm(bk[:, c4 * 128:(c4 + 1) * 128], xT[:, c, i * 128:(i + 1) * 128], ident,
                                   start=True, stop=True, bank=bB, reads=[xB, constB], transpose=True,
                                   wait_bank=(c4 == 0), set_bank=(c4 == 3))
                            cop(dve, lambda: nc.vector.tensor_copy(yo[i][:, g * 512:(g + 1) * 512], bk[:, :]),
                                reads=[bB], writes=[arB])
                        dma(pool, ch_y, out[t0 + i * 128:t0 + (i + 1) * 128, :], yo[i], reads=[arB], writes=[outB])
                barrier()
        barrier()
    return nc


def _host_consts():
    ident = np.eye(128, dtype=np.float32)
    ones = np.ones((128, 128), np.float32)
    p = np.arange(128)
    blk1 = (p[:, None] // 64 == p[None, :] // 64).astype(np.float32)
    rrot = np.zeros((128, 128), np.float32)
    for m in range(128):
        if m % 64 < 32:
            rrot[m + 32, m] = -1.0
        else:
            rrot[m - 32, m] = 1.0
    k = p[:, None]
    q = p[None, :]
    mcur = np.where(q >= k, 0.0, NEG).astype(np.float32)
    mprev = np.where(k > q, 0.0, NEG).astype(np.float32)
    tri = (k <= q).astype(np.float32)
    e65 = np.zeros((128, 128), np.float32)
    e65[64, 0:64] = 1.0
    cmat = np.concatenate([ident, ones, blk1, rrot, mcur, mprev, tri, e65], axis=1)
    inv_freq = (np.float32(10000.0) ** (-np.arange(0, HD, 2, dtype=np.float32) / np.float32(HD))).astype(np.float32)
    invf = inv_freq[p % 32][:, None].astype(np.float32)
    return np.ascontiguousarray(cmat), np.ascontiguousarray(invf)


def _in_maps(inputs):
    f = lambda a: np.ascontiguousarray(np.asarray(a, dtype=np.float32))
    col = lambda g: np.asarray(g, np.float32).reshape(-1, 128).T
    gcols = np.concatenate([
        col(inputs["norm_ffn1_g"][0]), col(inputs["norm_mix_g"][0]), col(inputs["norm_ffn2_g"][0]),
        col(np.concatenate([np.asarray(inputs["out_norm_fox_g"][0]), np.asarray(inputs["out_norm_swa_g"][0])]))], axis=1)
    qkg = np.stack([np.tile(np.asarray(inputs[k][0], np.float32), 2) for k in
                    ("fox_q_norm_g", "fox_k_norm_g", "swa_q_norm_g", "swa_k_norm_g")], axis=1)
    bfs = np.concatenate([np.broadcast_to(np.asarray(inputs["b_forget"][0], np.float32), (128, 16)),
                          np.broadcast_to(np.asarray(inputs["swa_sinks"][0], np.float32), (128, 16))], axis=1)
    cmat, invf = _host_consts()
    shared = {
        "wg1": f(inputs["ffn1_w_gate"][0]), "wu1": f(inputs["ffn1_w_up"][0]), "wd1": f(inputs["ffn1_w_down"][0]),
        "win": f(inputs["w_in"][0]), "wout": f(inputs["w_out"][0]),
        "wg2": f(inputs["ffn2_w_gate"][0]), "wu2": f(inputs["ffn2_w_up"][0]), "wd2": f(inputs["ffn2_w_down"][0]),
        "gcols": f(gcols), "qkg": f(qkg), "bfs": f(bfs), "cmat": cmat, "invf": invf,
    }
    xs = np.asarray(inputs["x"], np.float32)
    ps = np.asarray(inputs["positions"], np.int32)
    maps = []
    for b in range(NCORES):
        m = dict(shared)
        m["x"] = np.ascontiguousarray(xs[b])
        m["pos"] = np.ascontiguousarray(ps[b][None, :])
        maps.append(m)
    return maps


def kernel(**inputs):
    nc = build(stage=3, debug=False)
    maps = _in_maps(inputs)
    res = run_bass_kernel_spmd(nc, maps, core_ids=list(range(NCORES)))
    return np.stack([np.asarray(r["out"], np.float32) for r in res.results], axis=0)
```

```python
import contextlib
import numpy as np
import concourse.bass as bass
import concourse.mybir as mybir
from concourse.bass_utils import run_bass_kernel_spmd

F32 = mybir.dt.float32
I32 = mybir.dt.int32
F32R = mybir.dt.float32r
ALU = mybir.AluOpType
AF = mybir.ActivationFunctionType

S = 2048
D = 2048
DFF = 5632
NH = 16
HD = 64
NKV = 4
DIN = 4624
TT = 256
NT = S // TT
KC = D // 128
FC = DFF // 128
EPS = 1e-6
NCORES = 8
WCOLS = 256
NR = 3
WU = 128
FPARTS = ((0, 12), (12, 12), (24, 10), (34, 10))
NEG = -30000.0

O_QF, O_KF, O_VF, O_F, O_QS, O_KS, O_VS = 0, 1024, 2048, 3072, 3088, 4112, 4368


class Eng:
    def __init__(self, nc, eng, name, es):
        self.eng = eng
        self.sem = es.enter_context(nc.semaphore(name))
        self.cnt = 0
        self.step = 1
        self.seen = {}
        self.pending = []

    def wait(self, *toks):
        for t in toks:
            if t is None:
                continue
            src, v = t
            if self.seen.get(src, 0) >= v:
                continue
            if isinstance(src, Chan):
                v = src.cnt
            self.eng.wait_ge(src.sem, v * src.step)
            self.seen[src] = v

    def done(self, ins):
        self.cnt += 1
        ins.then_inc(self.sem, 1)
        return (self, self.cnt)


class Chan:
    def __init__(self, nc, name, es):
        self.sem = es.enter_context(nc.semaphore(name))
        self.cnt = 0
        self.step = 16


class Buf:
    def __init__(self, name, multi=False):
        self.name = name
        self.multi = multi
        self.w = {}
        self.r = {}
        self.wl = []

    def rtoks(self):
        return list(self.w.values()) + self.wl

    def wtoks(self):
        if self.multi:
            return list(self.r.values())
        return list(self.w.values()) + list(self.r.values())

    def set_w(self, tok):
        if self.multi:
            self.wl.append(tok)
        else:
            self.w = {tok[0]: tok}
            self.r = {}

    def set_r(self, tok):
        self.r[tok[0]] = tok


def build(stage=3, debug=False, ktiles=NT, kstop=None):
    nc = bass.Bass("TRN2", target_bir_lowering=False)

    def din(name, shape, dt=F32):
        return nc.dram_tensor(name, shape, dt, kind="ExternalInput").ap()

    skind = "ExternalOutput" if debug else "Internal"

    def dscr(name, shape):
        return nc.dram_tensor(name, shape, F32, kind=skind).ap()

    x = din("x", [S, D])
    pos = din("pos", [1, S], I32)
    wg1 = din("wg1", [D, DFF])
    wu1 = din("wu1", [D, DFF])
    wd1 = din("wd1", [DFF, D])
    win = din("win", [D, DIN])
    wout = din("wout", [D, D])
    wg2 = din("wg2", [D, DFF])
    wu2 = din("wu2", [D, DFF])
    wd2 = din("wd2", [DFF, D])
    gcols_d = din("gcols", [128, 64])
    qkg_d = din("qkg", [128, 4])
    bfs_d = din("bfs", [128, 32])
    cmat_d = din("cmat", [128, 8 * 128])
    invf_d = din("invf", [128, 1])
    out = nc.dram_tensor("out", [S, D], F32, kind="ExternalOutput").ap()

    h1T = dscr("h1T", [D, S])
    qf = dscr("qf", [NH, 65, S])
    kf = dscr("kf", [NH, 65, S])
    vf = dscr("vf", [S, 1024])
    lf = dscr("lf", [S, 16])
    qs = dscr("qs", [NH, 64, S])
    ks = dscr("ks", [NKV, 64, S])
    vs = dscr("vs", [S, 256])
    oT = dscr("oT", [D, S])
    cs = dscr("cs", [2, 128, S])

    with contextlib.ExitStack() as es:
        pe = Eng(nc, nc.tensor, "s_pe", es)
        act = Eng(nc, nc.scalar, "s_act", es)
        dve = Eng(nc, nc.vector, "s_dve", es)
        sync = Eng(nc, nc.sync, "s_sync", es)
        pool = Eng(nc, nc.gpsimd, "s_pool", es)
        engines = [pe, act, dve, sync, pool]
        chans = []

        def chan(name):
            c = Chan(nc, name, es)
            chans.append(c)
            return c

        def sbuf(stack, name, shape, dt=F32):
            return stack.enter_context(nc.sbuf_tensor("sb_" + name, shape, dt))

        def cop(E, fn, reads=(), writes=()):
            for b in reads:
                E.wait(*b.rtoks())
            for b in writes:
                E.wait(*b.wtoks())
            tok = E.done(fn())
            for b in reads:
                b.set_r(tok)
            for b in writes:
                b.set_w(tok)
            return tok

        def dma(Q, ch, out_ap, in_ap, reads=(), writes=()):
            for b in reads:
                Q.wait(*b.rtoks())
            for b in writes:
                Q.wait(*b.wtoks())
            Q.eng.dma_start(out=out_ap, in_=in_ap).then_inc(ch.sem, 16)
            ch.cnt += 1
            tok = (ch, ch.cnt)
            for b in reads:
                b.set_r(tok)
            for b in writes:
                b.set_w(tok)
            return tok

        def mm(out_ap, lhsT, rhs, start, stop, bank, reads=(), ms=None, transpose=False,
               wait_bank=None, set_bank=None):
            if wait_bank is None:
                wait_bank = start
            if set_bank is None:
                set_bank = stop
            if ms is None:
                ms = set_bank
            for b in reads:
                pe.wait(*b.rtoks())
            if wait_bank:
                pe.wait(*bank.wtoks())
            if transpose:
                ins = nc.tensor.transpose(out_ap, lhsT, rhs)
            else:
                ins = nc.tensor.matmul(out_ap, lhsT, rhs, start=start, stop=stop)
            pe.pending.extend(reads)
            tok = None
            if ms:
                tok = pe.done(ins)
                for b in pe.pending:
                    b.set_r(tok)
                pe.pending = []
            if set_bank:
                assert tok is not None
                bank.set_w(tok)
            return tok

        def barrier():
            assert not pe.pending
            toks = [(e, e.cnt) for e in (pe, act, dve) if e.cnt > 0]
            toks += [(c, c.cnt) for c in chans if c.cnt > 0]
            for e in engines:
                e.wait(*toks)

        banks = [es.enter_context(nc.psum_tensor(f"bank{i}", [128, 512], F32)) for i in range(8)]
        bankB = [Buf(f"bank{i}") for i in range(8)]
        ring = {"i": 0, "n": 8}

        def next_bank():
            b = ring["i"] % ring["n"]
            ring["i"] += 1
            assert not getattr(bankB[b], "held", False), "PSUM bank re-allocated while a pipelined unit still holds it"
            return banks[b], bankB[b]

        cmat = sbuf(es, "cmat", [128, 8 * 128])
        gcols = sbuf(es, "gcols", [128, 64])
        qkg = sbuf(es, "qkg", [128, 4])
        bfs = sbuf(es, "bfs", [128, 32])
        invf = sbuf(es, "invf", [128, 1])
        epsc = sbuf(es, "epsc", [128, 1])
        constB = Buf("const")
        ch_c = chan("c_const")
        dma(pool, ch_c, cmat[:], cmat_d[:, :], writes=[constB])
        dma(pool, ch_c, gcols[:], gcols_d[:, :], writes=[constB])
        dma(pool, ch_c, qkg[:], qkg_d[:, :], writes=[constB])
        dma(pool, ch_c, bfs[:], bfs_d[:, :], writes=[constB])
        dma(pool, ch_c, invf[:], invf_d[:, :], writes=[constB])
        cop(dve, lambda: nc.vector.memset(epsc[:], EPS), writes=[constB])
        cop(dve, lambda: nc.vector.tensor_scalar(qkg[:, 0:1], qkg[:, 0:1], 0.125, None, ALU.mult),
            reads=[constB], writes=[constB])
        cop(dve, lambda: nc.vector.tensor_scalar(qkg[:, 2:3], qkg[:, 2:3], 0.125, None, ALU.mult),
            reads=[constB], writes=[constB])
        cop(act, lambda: nc.scalar.activation(out=bfs[:, 16:32], in_=bfs[:, 16:32], func=AF.Exp),
            reads=[constB], writes=[constB])
        ident = cmat[:, 0:128]
        ones = cmat[:, 128:256]
        blk1 = cmat[:, 256:384]
        rrot = cmat[:, 384:512]
        mcur = cmat[:, 512:640]
        mprev = cmat[:, 640:768]
        tri = cmat[:, 768:896]
        e65 = cmat[:, 896:960]
        G_FFN1, G_MIX, G_FFN2, G_OUT = 0, 16, 32, 48

        whi = [sbuf(es, f"whi{i}", [128, 16, WU]) for i in range(2)]
        wlo = [sbuf(es, f"wlo{i}", [128, 16, WU]) for i in range(2)]
        whiB = [Buf(f"whi{i}") for i in range(2)]
        wloB = [Buf(f"wlo{i}") for i in range(2)]
        uh = sbuf(es, "uh", [128, KC, TT]); uhG = [Buf(f"uh{g}") for g in range(4)]
        ul = sbuf(es, "ul", [128, KC, TT]); ulG = [Buf(f"ul{g}") for g in range(4)]
        ah = sbuf(es, "ah", [128, 12, TT]); ahG = [Buf(f"ah{j}") for j in range(12)]
        al = sbuf(es, "al", [128, 12, TT]); alG = [Buf(f"al{j}") for j in range(12)]
        wch = [chan(f"c_w{i}") for i in range(NR)]
        rawB = [Buf(f"wraw{i}") for i in range(NR)]
        rawh = {"t": None}

        def R(ap):
            return ap.bitcast(F32R)

        def mm3(out_ap, l2, r2, start, stop, bank, reads):
            lh, ll = l2
            rh, rl = r2
            mm(out_ap, R(lh), R(rh), start=start, stop=False, bank=bank, reads=reads)
            mm(out_ap, R(lh), R(rl), start=False, stop=False, bank=bank)
            mm(out_ap, R(ll), R(rh), start=False, stop=stop, bank=bank)

        class WS:
            def __init__(self, descs):
                self.descs = descs
                self.units = [(i, h) for i, d in enumerate(descs) for h in range(max(1, d[4] // WU))]
                self.n_dma = 0
                self.n_prep = 0
                self.n_get = 0
                self.lim_d = 0
                self.lim_u = 0

            def open_phase(self, n_descs):
                self.lim_d += n_descs
                self.lim_u = sum(1 for (i, h) in self.units if i < self.lim_d)

            def _dma(self, i):
                w_ap, r0, nk, c0, ncols = self.descs[i]
                s_ = i % NR
                src = w_ap[r0:r0 + nk * 128, c0:c0 + ncols].rearrange("(k p) f -> p k f", p=128)
                dma(sync, wch[s_], rawh["t"][s_][:, 0:nk, 0:ncols], src, writes=[rawB[s_]])

            def _prep(self, u):
                i, h = self.units[u]
                w_ap, r0, nk, c0, ncols = self.descs[i]
                s_ = i % NR
                hs = u % 2
                cw = min(WU, ncols)
                src = rawh["t"][s_][:, 0:nk, h * WU:h * WU + cw]
                cop(act, lambda: nc.scalar.activation(out=R(whi[hs][:, 0:nk, 0:cw]), in_=src, func=AF.Copy),
                    reads=[rawB[s_]], writes=[whiB[hs]])
                cop(dve, lambda: nc.vector.tensor_tensor(out=R(wlo[hs][:, 0:nk, 0:cw]), in0=src,
                                                         in1=whi[hs][:, 0:nk, 0:cw], op=ALU.subtract),
                    reads=[rawB[s_], whiB[hs]], writes=[wloB[hs]])

            def get(self, desc):
                u = self.n_get
                i, h = self.units[u]
                assert self.descs[i][1:] == desc[1:] and self.descs[i][0] is desc[0], (u, i, desc[1:], self.descs[i][1:])
                while self.n_dma < min(self.lim_d, i + NR):
                    self._dma(self.n_dma)
                    self.n_dma += 1
                while self.n_prep < min(self.lim_u, u + 2):
                    self._prep(self.n_prep)
                    self.n_prep += 1
                self.n_get += 1
                hs = u % 2
                return whi[hs], wlo[hs], [whiB[hs], wloB[hs]]

        def ffn_descs(wg, wu, wd):
            out_ = []
            for (f0, nf) in FPARTS:
                for fp in range(nf // 2):
                    c0 = (f0 + 2 * fp) * 128
                    out_.append((wg, 0, KC, c0, 256))
                    out_.append((wu, 0, KC, c0, 256))
                for dp in range(D // 256):
                    out_.append((wd, f0 * 128, nf, dp * 256, 256))
            return out_

        def inproj_descs():
            out_ = []
            for base, n in ((O_QF, 4), (O_KF, 4), (O_VF, 4)):
                out_ += [(win, 0, KC, base + q4 * 256, 256) for q4 in range(n)]
            out_.append((win, 0, KC, O_F, 16))
            out_ += [(win, 0, KC, O_QS + q4 * 256, 256) for q4 in range(4)]
            out_.append((win, 0, KC, O_KS, 256))
            out_.append((win, 0, KC, O_VS, 256))
            return out_

        descsA = []
        descsC = []
        if stage >= 1:
            for t in range(NT):
                descsA += ffn_descs(wg1, wu1, wd1) + inproj_descs()
        if stage >= 3:
            for t in range(NT):
                descsC += [(wout, 0, KC, dp * 256, 256) for dp in range(D // 256)] + ffn_descs(wg2, wu2, wd2)
        ws = WS(descsA + descsC)

        csB = Buf("cs", multi=True)
        with contextlib.ExitStack() as ps:
            posi = sbuf(ps, "posi", [128, S], I32)
            ang = sbuf(ps, "ang", [128, S])
            tt_ = sbuf(ps, "tt_", [128, S])
            ki = sbuf(ps, "ki", [128, S], I32)
            kfl = sbuf(ps, "kfl", [128, S])
            res = sbuf(ps, "res", [128, S])
            pB, aB, tB, kB_, kfB, rB = (Buf(n) for n in ("posi", "ang", "tt", "ki", "kfl", "res"))
            ch_p = chan("c_pro")
            dma(pool, ch_p, posi[:], pos[0, :].partition_broadcast(128), writes=[pB])
            cop(dve, lambda: nc.vector.tensor_copy(ang[:], posi[:]), reads=[pB], writes=[aB])
            cop(dve, lambda: nc.vector.tensor_scalar(ang[:], ang[:], invf[:, 0:1], 1.0 / (2.0 * np.pi), ALU.mult, ALU.mult),
                reads=[aB, constB], writes=[aB])
            for which, shift in ((1, 0.0), (0, 0.25)):
                cop(dve, lambda: nc.vector.tensor_scalar(tt_[:], ang[:], shift, None, ALU.add), reads=[aB], writes=[tB])
                cop(dve, lambda: nc.vector.tensor_copy(ki[:], tt_[:]), reads=[tB], writes=[kB_])
                cop(dve, lambda: nc.vector.tensor_copy(kfl[:], ki[:]), reads=[kB_], writes=[kfB])
                cop(dve, lambda: nc.vector.tensor_tensor(out=tt_[:], in0=tt_[:], in1=kfl[:], op=ALU.subtract),
                    reads=[tB, kfB], writes=[tB])
                cop(act, lambda: nc.scalar.activation(out=res[:], in_=tt_[:], func=AF.Sin, scale=6.283185),
                    reads=[tB], writes=[rB])
                dma(pool, ch_p, cs[which, :, :], res[:], reads=[rB], writes=[csB])
            barrier()

        def rmsnorm3(xT, xB, sq, sqB, sd, sdB, tmpu, tmpuB, gofs, c_lo, c_hi, dim):
            n = c_hi - c_lo
            cop(act, lambda: nc.scalar.activation(out=sq[:, 0:n, :], in_=xT[:, c_lo:c_hi, :], func=AF.Square),
                reads=[xB], writes=[sqB])
            cop(dve, lambda: nc.vector.tensor_reduce(out=sd[:], in_=sq[:, 0:n, :].rearrange("p c t -> p t c"),
                                                     axis=mybir.AxisListType.X, op=ALU.add),
                reads=[sqB], writes=[sdB])
            bk, bB = next_bank()
            mm(bk[:, 0:TT], ones, sd[:], start=True, stop=True, bank=bB, reads=[sdB, constB])
            cop(act, lambda: nc.scalar.activation(out=sd[:], in_=bk[:, 0:TT], func=AF.Sqrt, bias=epsc[:, 0:1], scale=1.0 / dim),
                reads=[bB, constB], writes=[sdB])
            cop(dve, lambda: nc.vector.reciprocal(sd[:], sd[:]), reads=[sdB], writes=[sdB])
            for g in range(n // 4):
                tb, tbB = tmpu[g % 2], tmpuB[g % 2]
                c0 = c_lo + 4 * g
                for c4 in range(4):
                    c = c0 + c4
                    cop(dve, lambda: nc.vector.scalar_tensor_tensor(
                        out=tb[:, c4, :], in0=xT[:, c, :], scalar=gcols[:, gofs + c:gofs + c + 1], in1=sd[:],
                        op0=ALU.mult, op1=ALU.mult), reads=[xB, sdB, constB], writes=[tbB])
                cop(act, lambda: nc.scalar.activation(out=R(uh[:, c0:c0 + 4, :]), in_=tb[:], func=AF.Copy),
                    reads=[tbB], writes=[uhG[c0 // 4]])
                cop(dve, lambda: nc.vector.tensor_tensor(out=R(ul[:, c0:c0 + 4, :]), in0=tb[:], in1=uh[:, c0:c0 + 4, :],
                                                         op=ALU.subtract), reads=[tbB, uhG[c0 // 4]], writes=[ulG[c0 // 4]])

        def ffn3(xT, xB, sgs, sgB, tmpa, tmpaB, wg, wu, wd):
            for (f0, nf) in FPARTS:
                for fp in range(nf // 2):
                    c0 = (f0 + 2 * fp) * 128
                    gb = []
                    for wmat in (wg, wu):
                        for half in range(2):
                            wh_, wl_, wB_ = ws.get((wmat, 0, KC, c0, 256))
                            bk, bB = next_bank()
                            for k in range(KC):
                                mm3(bk[:, 0:TT], (wh_[:, k, :], wl_[:, k, :]), (uh[:, k, :], ul[:, k, :]),
                                    start=(k == 0), stop=(k == KC - 1), bank=bB, reads=wB_ + [uhG[k // 4], ulG[k // 4]])
                            gb.append((bk, bB))
                    for half in range(2):
                        j = 2 * fp + half
                        g_, gB_ = gb[half]
                        u_, uB_ = gb[2 + half]
                        sg, sB = sgs[j % 2], sgB[j % 2]
                        ta, taB = tmpa[j % 2], tmpaB[j % 2]
                        cop(act, lambda: nc.scalar.activation(out=sg[:], in_=g_[:, 0:TT], func=AF.Silu),
                            reads=[gB_], writes=[sB])
                        cop(dve, lambda: nc.vector.tensor_tensor(out=ta[:], in0=sg[:], in1=u_[:, 0:TT], op=ALU.mult),
                            reads=[sB, uB_], writes=[taB])
                        cop(act, lambda: nc.scalar.activation(out=R(ah[:, j, :]), in_=ta[:], func=AF.Copy),
                            reads=[taB], writes=[ahG[j]])
                        cop(dve, lambda: nc.vector.tensor_tensor(out=R(al[:, j, :]), in0=ta[:], in1=ah[:, j, :], op=ALU.subtract),
                            reads=[taB, ahG[j]], writes=[alG[j]])
                for dp in range(D // 256):
                    for half in range(2):
                        dm = 2 * dp + half
                        wh_, wl_, wB_ = ws.get((wd, f0 * 128, nf, dp * 256, 256))
                        bk, bB = next_bank()
                        for j in range(nf):
                            mm3(bk[:, 0:TT], (wh_[:, j, :], wl_[:, j, :]), (ah[:, j, :], al[:, j, :]),
                                start=(j == 0), stop=(j == nf - 1), bank=bB, reads=wB_ + [ahG[j], alG[j]])
                        cop(dve, lambda: nc.vector.scalar_tensor_tensor(
                            out=xT[:, dm, :], in0=bk[:, 0:TT], scalar=0.5, in1=xT[:, dm, :],
                            op0=ALU.mult, op1=ALU.add), reads=[bB, xB], writes=[xB])

        h1B = Buf("h1T", multi=True)
        qfB, kfB2, vfB, lfB, qsB, ksB, vsB, oTB = (Buf(n, multi=True) for n in
                                                   ("qf", "kf", "vf", "lf", "qs", "ks", "vs", "oT"))

        with contextlib.ExitStack() as pa:
            raw = [sbuf(pa, f"wraw{i}", [128, 16, 256]) for i in range(NR)]
            rawh["t"] = raw
            ws.open_phase(len(descsA))
            xT = sbuf(pa, "xT", [128, KC, TT]); xB = Buf("xT")
            arena = sbuf(pa, "arena", [128, KC * TT]); arB = Buf("arena")
            xin = [arena[:, 0:D], arena[:, D:2 * D]]
            sq = arena[:, :].rearrange("p (c t) -> p c t", c=KC)
            tmpu = [sbuf(pa, f"tmpu{i}", [128, 4, TT]) for i in range(2)]
            tmpuB = [Buf(f"tmpu{i}") for i in range(2)]
            tmpa = [sbuf(pa, f"tmpa{i}", [128, TT]) for i in range(2)]
            tmpaB = [Buf(f"tmpa{i}") for i in range(2)]
            sd = sbuf(pa, "sd", [128, TT]); sdB = Buf("sd")
            sgs = [sbuf(pa, f"sg{i}", [128, TT]) for i in range(2)]
            sgB = [Buf(f"sg{i}") for i in range(2)]
            cst = sbuf(pa, "cst", [128, 2, TT]); cstB = Buf("cst")
            qn = [sbuf(pa, f"qn{i}", [128, TT]) for i in range(2)]
            qnB = [Buf(f"qn{i}") for i in range(2)]
            qo = [sbuf(pa, f"qo{i}", [128, TT]) for i in range(2)]
            qoB = [Buf(f"qo{i}") for i in range(2)]
            t1 = sbuf(pa, "t1", [128, TT]); t1B = Buf("t1")
            sq2 = sbuf(pa, "sq2", [128, TT]); sq2B = Buf("sq2")
            sd2 = sbuf(pa, "sd2", [128, TT]); sd2B = Buf("sd2")
            vT = [sbuf(pa, f"vT{i}", [128, TT]) for i in range(2)]
            vTB = [Buf(f"vT{i}") for i in range(2)]
            vst = [sbuf(pa, f"vst{i}", [128, 2, 128]) for i in range(2)]
            vstB = [Buf(f"vst{i}") for i in range(2)]
            fT = sbuf(pa, "fT", [16, TT]); fTB = Buf("fT")
            lft = [sbuf(pa, f"lft{i}", [128, 16]) for i in range(2)]
            lftB = [Buf(f"lft{i}") for i in range(2)]
            ch_x = chan("c_xin")
            ch_xa = chan("c_xin_act")
            ch_h = chan("c_h1")
            ch_cs = chan("c_cs")
            ch_q = [chan(f"c_qo{i}") for i in range(2)]
            ch_v = [chan(f"c_v{i}") for i in range(2)]
            ch_l = chan("c_lf")
            tog = {"q": 0, "v": 0}

            n_tiles = min(NT, ktiles) if stage >= 1 else 0
            for t in range(n_tiles):
                t0 = t * TT
                if t == 0:
                    for i in range(2):
                        dma(pool, ch_x, xin[i], x[t0 + i * 128:t0 + (i + 1) * 128, :], writes=[arB])
                for i in range(2):
                    for g in range(4):
                        bk, bB = next_bank()
                        for c4 in range(4):
                            c = 4 * g + c4
                            mm(bk[:, c4 * 128:(c4 + 1) * 128], xin[i][:, c * 128:(c + 1) * 128], ident,
                               start=True, stop=True, bank=bB, reads=[arB, constB], transpose=True,
                               wait_bank=(c4 == 0), set_bank=(c4 == 3))
                        cop(dve, lambda: nc.vector.tensor_copy(
                            xT[:, 4 * g:4 * g + 4, i * 128:(i + 1) * 128],
                            bk[:, :].rearrange("p (c n) -> p c n", c=4)), reads=[bB], writes=[xB])
                rmsnorm3(xT, xB, sq, arB, sd, sdB, tmpu, tmpuB, G_FFN1, 0, KC, D)
                if kstop == 'norm':
                    break
                ffn3(xT, xB, sgs, sgB, tmpa, tmpaB, wg1, wu1, wd1)
                if kstop == 'ffn':
                    break
                dma(pool, ch_h, h1T[:, t0:t0 + TT].rearrange("(c p) t -> p c t", p=128), xT[:], reads=[xB], writes=[h1B])
                rmsnorm3(xT, xB, sq, arB, sd, sdB, tmpu, tmpuB, G_MIX, 0, KC, D)
                dma(pool, ch_cs, cst[:], cs[:, :, t0:t0 + TT].rearrange("a p t -> p a t"), reads=[csB], writes=[cstB])

                def proj_unit(desc):
                    wh_, wl_, wB_ = ws.get(desc)
                    m = min(128, desc[4])
                    pb, pB2 = next_bank()
                    for k in range(KC):
                        mm3(pb[0:m, 0:TT], (wh_[:, k, 0:m], wl_[:, k, 0:m]), (uh[:, k, :], ul[:, k, :]),
                            start=(k == 0), stop=(k == KC - 1), bank=pB2, reads=wB_ + [uhG[k // 4], ulG[k // 4]])
                    return pb, pB2

                def qk_unit(c0, cc, gcol, rope, dst, dstB, h0):
                    pb, pB2 = proj_unit((win, 0, KC, c0, 256))
                    pB2.held = True
                    yield
                    cop(act, lambda: nc.scalar.activation(out=sq2[:], in_=pb[:, 0:TT], func=AF.Square),
                        reads=[pB2], writes=[sq2B])
                    sb_, sB2 = next_bank()
                    sB2.held = True
                    mm(sb_[:, 0:TT], blk1, sq2[:], start=True, stop=True, bank=sB2, reads=[sq2B, constB])
                    yield
                    cop(act, lambda: nc.scalar.activation(out=sd2[:], in_=sb_[:, 0:TT], func=AF.Sqrt,
                                                          bias=epsc[:, 0:1], scale=1.0 / HD),
                        reads=[sB2, constB], writes=[sd2B])
                    sB2.held = False
                    cop(dve, lambda: nc.vector.reciprocal(sd2[:], sd2[:]), reads=[sd2B], writes=[sd2B])
                    tog["q"] += 1
                    ix = tog["q"] % 2
                    o_, oB_, och = qo[ix], qoB[ix], ch_q[ix]
                    if not rope:
                        cop(dve, lambda: nc.vector.scalar_tensor_tensor(
                            out=o_[:], in0=pb[:, 0:TT], scalar=qkg[:, gcol:gcol + 1], in1=sd2[:],
                            op0=ALU.mult, op1=ALU.mult), reads=[pB2, sd2B, constB], writes=[oB_])
                        pB2.held = False
                    else:
                        n_, nB_ = qn[ix], qnB[ix]
                        cop(dve, lambda: nc.vector.scalar_tensor_tensor(
                            out=n_[:], in0=pb[:, 0:TT], scalar=qkg[:, gcol:gcol + 1], in1=sd2[:],
                            op0=ALU.mult, op1=ALU.mult), reads=[pB2, sd2B, constB], writes=[nB_])
                        pB2.held = False
                        rb, rB2 = next_bank()
                        rB2.held = True
                        mm(rb[:, 0:TT], rrot, n_[:], start=True, stop=True, bank=rB2, reads=[nB_, constB])
                        yield
                        cop(dve, lambda: nc.vector.tensor_tensor(out=t1[:], in0=rb[:, 0:TT], in1=cst[:, 1, :], op=ALU.mult),
                            reads=[rB2, cstB], writes=[t1B])
                        rB2.held = False
                        cop(dve, lambda: nc.vector.tensor_tensor(out=o_[:], in0=n_[:], in1=cst[:, 0, :], op=ALU.mult),
                            reads=[nB_, cstB], writes=[oB_])
                        cop(dve, lambda: nc.vector.tensor_tensor(out=o_[:], in0=o_[:], in1=t1[:], op=ALU.add),
                            reads=[oB_, t1B], writes=[oB_])
                    for hh in range(2):
                        h = h0 + 2 * cc + hh
                        dma(pool, och, dst[h, 0:64, t0:t0 + TT], o_[hh * 64:(hh + 1) * 64, :], reads=[oB_], writes=[dstB])

                def v_unit(c0, cc, dst, dstB, d0):
                    pb, pB2 = proj_unit((win, 0, KC, c0, 256))
                    pB2.held = True
                    yield
                    tog["v"] += 1
                    ix = tog["v"] % 2
                    cop(act, lambda: nc.scalar.copy(out=vT[ix][:], in_=pb[:, 0:TT]), reads=[pB2], writes=[vTB[ix]])
                    pB2.held = False
                    yield
                    tb_, tB_ = next_bank()
                    tB_.held = True
                    for i in range(2):
                        mm(tb_[:, i * 128:(i + 1) * 128], vT[ix][:, i * 128:(i + 1) * 128], ident,
                           start=True, stop=True, bank=tB_, reads=[vTB[ix], constB], transpose=True,
                           wait_bank=(i == 0), set_bank=(i == 1))
                    yield
                    cop(dve, lambda: nc.vector.tensor_copy(vst[ix][:], tb_[:, 0:256].rearrange("p (i n) -> p i n", i=2)),
                        reads=[tB_], writes=[vstB[ix]])
                    tB_.held = False
                    for i in range(2):
                        dma(pool, ch_v[ix], dst[t0 + i * 128:t0 + (i + 1) * 128, d0 + cc * 128:d0 + (cc + 1) * 128],
                            vst[ix][:, i, :], reads=[vstB[ix]], writes=[dstB])

                def f_unit():
                    pb, pB2 = proj_unit((win, 0, KC, O_F, 16))
                    pB2.held = True
                    yield
                    cop(act, lambda: nc.scalar.copy(out=fT[0:16, :], in_=pb[0:16, 0:TT]), reads=[pB2], writes=[fTB])
                    pB2.held = False
                    yield
                    tbs = []
                    for i in range(2):
                        tb_, tB_ = next_bank()
                        tB_.held = True
                        mm(tb_[:, 0:16], fT[0:16, i * 128:(i + 1) * 128], cmat[0:16, 0:16], start=True, stop=True, bank=tB_,
                           reads=[fTB, constB])
                        tbs.append((tb_, tB_))
                    yield
                    for i in range(2):
                        tb_, tB_ = tbs[i]
                        cop(dve, lambda: nc.vector.tensor_tensor(out=lft[i][:], in0=tb_[:, 0:16], in1=bfs[:, 0:16], op=ALU.add),
                            reads=[tB_, constB], writes=[lftB[i]])
                        tB_.held = False
                        cop(act, lambda: nc.scalar.activation(out=lft[i][:], in_=lft[i][:], func=AF.Sigmoid),
                            reads=[lftB[i]], writes=[lftB[i]])
                        cop(act, lambda: nc.scalar.activation(out=lft[i][:], in_=lft[i][:], func=AF.Ln),
                            reads=[lftB[i]], writes=[lftB[i]])
                        dma(pool, ch_l, lf[t0 + i * 128:t0 + (i + 1) * 128, :], lft[i][:], reads=[lftB[i]], writes=[lfB])

                units = []
                for q4 in range(4):
                    units += [qk_unit(O_QF + q4 * 256, cc, 0, False, qf, qfB, 4 * q4) for cc in range(2)]
                for q4 in range(4):
                    units += [qk_unit(O_KF + q4 * 256, cc, 1, False, kf, kfB2, 4 * q4) for cc in range(2)]
                for q4 in range(4):
                    units += [v_unit(O_VF + q4 * 256, cc, vf, vfB, q4 * 256) for cc in range(2)]
                units.append(f_unit())
                for q4 in range(4):
                    units += [qk_unit(O_QS + q4 * 256, cc, 2, True, qs, qsB, 4 * q4) for cc in range(2)]
                units += [qk_unit(O_KS, cc, 3, True, ks, ksB, 0) for cc in range(2)]
                units += [v_unit(O_VS, cc, vs, vsB, 0) for cc in range(2)]
                active = []
                for g_ in units:
                    next(g_)
                    for h_ in list(active):
                        try:
                            next(h_)
                        except StopIteration:
                            active.remove(h_)
                    active.append(g_)
                while active:
                    for h_ in list(active):
                        try:
                            next(h_)
                        except StopIteration:
                            active.remove(h_)
                if t + 1 < n_tiles:
                    for i in range(2):
                        dma(act, ch_xa, xin[i], x[t0 + TT + i * 128:t0 + TT + (i + 1) * 128, :], writes=[arB])
            barrier()

        if stage >= 2:
            ring["n"] = 6
            ring["i"] = 0
            with contextlib.ExitStack() as pb_:
                lfs = sbuf(pb_, "lfs", [128, 16, 16]); lfsB = Buf("lfs")
                csb = sbuf(pb_, "csb", [128, 16, 16]); csbB = Buf("csb")
                negc = sbuf(pb_, "negc", [128, 16, 16]); negcB = Buf("negc")
                cT = sbuf(pb_, "cT", [16, S]); cTB = Buf("cT")
                onesrow = sbuf(pb_, "onesrow", [1, S]); orB = Buf("onesrow")
                qa = [sbuf(pb_, f"qa{i}", [65, S]) for i in range(2)]
                qaB = [Buf(f"qa{i}") for i in range(2)]
                ka = [sbuf(pb_, f"ka{i}", [65, S]) for i in range(2)]
                kaB = [Buf(f"ka{i}") for i in range(2)]
                va = [sbuf(pb_, f"va{i}", [128, 16, 65]) for i in range(2)]
                vaB = [Buf(f"va{i}") for i in range(2)]
                pt = [sbuf(pb_, f"pt{i}", [128, 512]) for i in range(3)]
                ptB = [Buf(f"pt{i}") for i in range(3)]
                accs = [sbuf(pb_, f"accs{i}", [65, 512]) for i in range(2)]
                accsB = [Buf(f"accs{i}") for i in range(2)]
                rden = sbuf(pb_, "rden", [64, 512]); rdenB = Buf("rden")
                ost = [sbuf(pb_, f"ost{i}", [64, 512]) for i in range(2)]
                ostB = [Buf(f"ost{i}") for i in range(2)]
                ch_b = chan("c_bpro")
                ch_qa = [chan(f"c_qa{i}") for i in range(2)]
                ch_ka = [chan(f"c_ka{i}") for i in range(2)]
                ch_va = [chan(f"c_va{i}") for i in range(2)]
                ch_o = [chan(f"c_ost{i}") for i in range(2)]

                dma(pool, ch_b, lfs[:], lf[:, :].rearrange("(j p) h -> p j h", p=128), reads=[lfB], writes=[lfsB])
                cop(dve, lambda: nc.vector.memset(onesrow[:], 1.0), writes=[orB])
                for i in range(2):
                    cop(dve, lambda: nc.vector.memset(va[i][:, :, 64:65], 1.0), writes=[vaB[i]])
                for j in range(16):
                    bk, bB = next_bank()
                    for i in range(j):
                        mm(bk[:, 0:16], ones, lfs[:, i, :], start=(i == 0), stop=False, bank=bB, reads=[lfsB, constB])
                    mm(bk[:, 0:16], tri, lfs[:, j, :], start=(j == 0), stop=True, bank=bB, reads=[lfsB, constB])
                    cop(dve, lambda: nc.vector.tensor_copy(csb[:, j, :], bk[:, 0:16]), reads=[bB], writes=[csbB])
                    cop(dve, lambda: nc.vector.tensor_scalar(negc[:, j, :], bk[:, 0:16], -1.0, None, ALU.mult),
                        reads=[bB], writes=[negcB])
                    b2, b2B = next_bank()
                    mm(b2[0:16, 0:128], csb[:, j, :], ident, start=True, stop=True, bank=b2B, reads=[csbB, constB])
                    cop(dve, lambda: nc.vector.tensor_copy(cT[0:16, j * 128:(j + 1) * 128], b2[0:16, 0:128]),
                        reads=[b2B], writes=[cTB])
                for h in range(NH):
                    dma(pool, ch_b, qf[h, 64:65, :], cT[h:h + 1, :], reads=[cTB], writes=[qfB])
                    dma(pool, ch_b, kf[h, 64:65, :], onesrow[0:1, :], reads=[orB], writes=[kfB2])

                accbank = [(banks[6], bankB[6]), (banks[7], bankB[7])]
                cnt = {"acc": 0, "pt": 0}
                deferred = []

                def finalize(h_row, qc, acc, accB_, sink_col, now):
                    ix = cnt["acc"] % 2
                    cnt["acc"] += 1
                    a_, aB_ = accs[ix], accsB[ix]
                    cop(dve, lambda: nc.vector.tensor_copy(a_[0:65, :], acc[0:65, :]), reads=[accB_], writes=[aB_])

                    def tail():
                        db, dB_ = next_bank()
                        mm(db[0:64, 0:512], e65[0:65, :], a_[0:65, :], start=True, stop=True, bank=dB_, reads=[aB_, constB])
                        if sink_col is None:
                            cop(dve, lambda: nc.vector.reciprocal(rden[0:64, :], db[0:64, :]), reads=[dB_], writes=[rdenB])
                        else:
                            cop(dve, lambda: nc.vector.tensor_scalar(rden[0:64, :], db[0:64, :],
                                                                     bfs[0:64, 16 + sink_col:17 + sink_col], None, ALU.add),
                                reads=[dB_, constB], writes=[rdenB])
                            cop(dve, lambda: nc.vector.reciprocal(rden[0:64, :], rden[0:64, :]), reads=[rdenB], writes=[rdenB])
                        o_, oB_ = ost[ix], ostB[ix]
                        cop(dve, lambda: nc.vector.tensor_tensor(out=o_[0:64, :], in0=a_[0:64, :], in1=rden[0:64, :], op=ALU.mult),
                            reads=[aB_, rdenB], writes=[oB_])
                        dma(pool, ch_o[ix], oT[h_row:h_row + 64, qc * 512:(qc + 1) * 512], o_[0:64, :], reads=[oB_], writes=[oTB])
                    deferred.append((now + 2, tail))

                def load_head(kind, h):
                    ib = h % 2
                    if kind == "fox":
                        dma(pool, ch_qa[ib], qa[ib][0:65, :], qf[h, :, :], reads=[qfB], writes=[qaB[ib]])
                        dma(pool, ch_ka[ib], ka[ib][0:65, :], kf[h, :, :], reads=[kfB2], writes=[kaB[ib]])
                        dma(pool, ch_va[ib], va[ib][:, :, 0:64],
                            vf[:, h * 64:(h + 1) * 64].rearrange("(j p) d -> p j d", p=128), reads=[vfB], writes=[vaB[ib]])
                    else:
                        kv = h // 4
                        dma(pool, ch_qa[ib], qa[ib][0:64, :], qs[h, :, :], reads=[qsB], writes=[qaB[ib]])
                        dma(pool, ch_ka[ib], ka[ib][0:64, :], ks[kv, :, :], reads=[ksB], writes=[kaB[ib]])
                        dma(pool, ch_va[ib], va[ib][:, :, 0:64],
                            vs[:, kv * 64:(kv + 1) * 64].rearrange("(j p) d -> p j d", p=128), reads=[vsB], writes=[vaB[ib]])

                heads = [("fox", h) for h in range(NH)] + [("swa", h) for h in range(NH)]

                def fox_step(h, qc, j, prefetch):
                    ib = h % 2
                    nj = 4 * qc + 4
                    q0 = max(512 * qc, 128 * j)
                    n = 512 * (qc + 1) - q0
                    off = q0 - 512 * qc
                    diag = j >= 4 * qc
                    st = {}

                    def s_part():
                        if prefetch is not None:
                            load_head(*prefetch)
                        sbk, sB_ = next_bank()
                        st["s"] = (sbk, sB_)
                        mm(sbk[:, 0:n], ka[ib][0:65, j * 128:(j + 1) * 128], qa[ib][0:65, q0:q0 + n],
                           start=True, stop=(not diag), bank=sB_, reads=[kaB[ib], qaB[ib]])
                        if diag:
                            mm(sbk[:, 0:128], ident, mcur, start=False, stop=True, bank=sB_, reads=[constB])

                    def r_part(now, acc_holder):
                        sbk, sB_ = st["s"]
                        acc, accB_ = acc_holder["acc"]
                        ip = cnt["pt"] % 3
                        cnt["pt"] += 1
                        p_, pB_ = pt[ip], ptB[ip]
                        cop(act, lambda: nc.scalar.activation(out=p_[:, 0:n], in_=sbk[:, 0:n], func=AF.Exp,
                                                              bias=negc[:, j, h:h + 1], scale=1.0),
                            reads=[sB_, negcB], writes=[pB_])
                        mm(acc[0:65, off:512], va[ib][:, j, 0:65], p_[:, 0:n], start=(j == 0), stop=(j == nj - 1),
                           bank=accB_, reads=[vaB[ib], pB_], ms=True)
                        if j == nj - 1:
                            finalize(h * 64, qc, acc, accB_, None, now)
                    return s_part, r_part, (j == 0)

                def swa_step(h, qc, qb4, prefetch):
                    ib = h % 2
                    qb = 4 * qc + qb4
                    kbs = [kb for kb in (qb, qb - 1) if kb >= 0]
                    nn = 128 * len(kbs)
                    st = {}

                    def s_part():
                        if prefetch is not None:
                            load_head(*prefetch)
                        sbk, sB_ = next_bank()
                        st["s"] = (sbk, sB_)
                        for i, kb in enumerate(kbs):
                            mm(sbk[:, i * 128:(i + 1) * 128], ka[ib][0:64, kb * 128:(kb + 1) * 128],
                               qa[ib][0:64, qb * 128:(qb + 1) * 128], start=(i == 0), stop=False, bank=sB_,
                               reads=[kaB[ib], qaB[ib]], wait_bank=(i == 0), set_bank=False, ms=False)
                        mm(sbk[:, 0:nn], ident, cmat[:, 512:512 + nn], start=False, stop=True, bank=sB_,
                           reads=[constB], wait_bank=False, set_bank=True)

                    def r_part(now, acc_holder):
                        sbk, sB_ = st["s"]
                        acc, accB_ = acc_holder["acc"]
                        ip = cnt["pt"] % 3
                        cnt["pt"] += 1
                        p_, pB_ = pt[ip], ptB[ip]
                        cop(act, lambda: nc.scalar.activation(out=p_[:, 0:nn], in_=sbk[:, 0:nn], func=AF.Exp),
                            reads=[sB_], writes=[pB_])
                        for i, kb in enumerate(kbs):
                            first = (qb4 == 0 and i == 0)
                            last = (qb4 == 3 and i == len(kbs) - 1)
                            mm(acc[0:65, qb4 * 128:(qb4 + 1) * 128], va[ib][:, kb, 0:65], p_[:, i * 128:(i + 1) * 128],
                               start=(i == 0), stop=(i == len(kbs) - 1), bank=accB_, reads=[vaB[ib], pB_],
                               ms=True, wait_bank=first, set_bank=last)
                        if qb4 == 3:
                            finalize(1024 + h * 64, qc, acc, accB_, h, now)
                    return s_part, r_part, (qb4 == 0)

                steps = []
                for hi, (kind, h) in enumerate(heads):
                    nxt = heads[hi + 1] if hi + 1 < len(heads) else None
                    for qc in range(4):
                        inner = range(4 * qc + 4) if kind == "fox" else range(4)
                        for jj in inner:
                            pf = nxt if (qc == 1 and jj == 0) else None
                            steps.append((fox_step if kind == "fox" else swa_step)(h, qc, jj, pf))
                load_head(*heads[0])
                LA = 2
                acc_holder = {}
                for k in range(min(LA, len(steps))):
                    steps[k][0]()
                for i, (s_part, r_part, newacc) in enumerate(steps):
                    if newacc:
                        acc_holder["acc"] = accbank[cnt["acc"] % 2]
                    while deferred and deferred[0][0] <= i:
                        deferred.pop(0)[1]()
                    if i + LA < len(steps):
                        steps[i + LA][0]()
                    r_part(i, acc_holder)
                while deferred:
                    deferred.pop(0)[1]()
                barrier()
            ring["n"] = 8
            ring["i"] = 0

        if stage >= 3:
            with contextlib.ExitStack() as pc:
                raw = [sbuf(pc, f"wrawc{i}", [128, 16, 256]) for i in range(NR)]
                rawh["t"] = raw
                ws.open_phase(len(descsC))
                xT = sbuf(pc, "xTc", [128, KC, TT]); xB = Buf("xTc")
                oin = sbuf(pc, "oin", [128, KC, TT]); oinB = Buf("oin")
                arena = sbuf(pc, "arenac", [128, KC * TT]); arB = Buf("arenac")
                yo = [arena[:, 0:D], arena[:, D:2 * D]]
                sq = arena[:, :].rearrange("p (c t) -> p c t", c=KC)
                tmpu = [sbuf(pc, f"tmpuc{i}", [128, 4, TT]) for i in range(2)]
                tmpuB = [Buf(f"tmpuc{i}") for i in range(2)]
                tmpa = [sbuf(pc, f"tmpac{i}", [128, TT]) for i in range(2)]
                tmpaB = [Buf(f"tmpac{i}") for i in range(2)]
                sd = sbuf(pc, "sdc", [128, TT]); sdB = Buf("sdc")
                sgs = [sbuf(pc, f"sgc{i}", [128, TT]) for i in range(2)]
                sgB = [Buf(f"sgc{i}") for i in range(2)]
                ch_li = chan("c_cin")
                ch_oi = chan("c_oin")
                ch_oia = chan("c_oin_act")
                ch_y = chan("c_yo")
                outB = Buf("out", multi=True)
                for t in range(NT):
                    t0 = t * TT
                    if t == 0:
                        dma(pool, ch_oi, oin[:], oT[:, t0:t0 + TT].rearrange("(c p) t -> p c t", p=128), reads=[oTB], writes=[oinB])
                    dma(pool, ch_li, xT[:], h1T[:, t0:t0 + TT].rearrange("(c p) t -> p c t", p=128), reads=[h1B], writes=[xB])
                    rmsnorm3(oin, oinB, sq, arB, sd, sdB, tmpu, tmpuB, G_OUT, 0, 8, 1024)
                    rmsnorm3(oin, oinB, sq, arB, sd, sdB, tmpu, tmpuB, G_OUT, 8, 16, 1024)
                    if t + 1 < NT:
                        dma(act, ch_oia, oin[:], oT[:, t0 + TT:t0 + 2 * TT].rearrange("(c p) t -> p c t", p=128),
                            reads=[oTB], writes=[oinB])
                    for dp in range(D // 256):
                        for half in range(2):
                            dm = 2 * dp + half
                            wh_, wl_, wB_ = ws.get((wout, 0, KC, dp * 256, 256))
                            pb, pB2 = next_bank()
                            for k in range(KC):
                                mm3(pb[:, 0:TT], (wh_[:, k, :], wl_[:, k, :]), (uh[:, k, :], ul[:, k, :]),
                                    start=(k == 0), stop=(k == KC - 1), bank=pB2, reads=wB_ + [uhG[k // 4], ulG[k // 4]])
                            cop(dve, lambda: nc.vector.tensor_tensor(out=xT[:, dm, :], in0=pb[:, 0:TT], in1=xT[:, dm, :], op=ALU.add),
                                reads=[pB2, xB], writes=[xB])
                    rmsnorm3(xT, xB, sq, arB, sd, sdB, tmpu, tmpuB, G_FFN2, 0, KC, D)
                    ffn3(xT, xB, sgs, sgB, tmpa, tmpaB, wg2, wu2, wd2)
                    for i in range(2):
                        for g in range(4):
                            bk, bB = next_bank()
                            for c4 in range(4):
                                c = 4 * g + c4
                                mm(bk[:, c4 * 128:(c4 + 1) * 128], xT[:, c, i * 128:(i + 1) * 128], ident,
                                   start=True, stop=True, bank=bB, reads=[xB, constB], transpose=True,
                                   wait_bank=(c4 == 0), set_bank=(c4 == 3))
                            cop(dve, lambda: nc.vector.tensor_copy(yo[i][:, g * 512:(g + 1) * 512], bk[:, :]),
                                reads=[bB], writes=[arB])
                        dma(pool, ch_y, out[t0 + i * 128:t0 + (i + 1) * 128, :], yo[i], reads=[arB], writes=[outB])
                barrier()
        barrier()
    return nc


def _host_consts():
    ident = np.eye(128, dtype=np.float32)
    ones = np.ones((128, 128), np.float32)
    p = np.arange(128)
    blk1 = (p[:, None] // 64 == p[None, :] // 64).astype(np.float32)
    rrot = np.zeros((128, 128), np.float32)
    for m in range(128):
        if m % 64 < 32:
            rrot[m + 32, m] = -1.0
        else:
            rrot[m - 32, m] = 1.0
    k = p[:, None]
    q = p[None, :]
    mcur = np.where(q >= k, 0.0, NEG).astype(np.float32)
    mprev = np.where(k > q, 0.0, NEG).astype(np.float32)
    tri = (k <= q).astype(np.float32)
    e65 = np.zeros((128, 128), np.float32)
    e65[64, 0:64] = 1.0
    cmat = np.concatenate([ident, ones, blk1, rrot, mcur, mprev, tri, e65], axis=1)
    inv_freq = (np.float32(10000.0) ** (-np.arange(0, HD, 2, dtype=np.float32) / np.float32(HD))).astype(np.float32)
    invf = inv_freq[p % 32][:, None].astype(np.float32)
    return np.ascontiguousarray(cmat), np.ascontiguousarray(invf)


def _in_maps(inputs):
    f = lambda a: np.ascontiguousarray(np.asarray(a, dtype=np.float32))
    col = lambda g: np.asarray(g, np.float32).reshape(-1, 128).T
    gcols = np.concatenate([
        col(inputs["norm_ffn1_g"][0]), col(inputs["norm_mix_g"][0]), col(inputs["norm_ffn2_g"][0]),
        col(np.concatenate([np.asarray(inputs["out_norm_fox_g"][0]), np.asarray(inputs["out_norm_swa_g"][0])]))], axis=1)
    qkg = np.stack([np.tile(np.asarray(inputs[k][0], np.float32), 2) for k in
                    ("fox_q_norm_g", "fox_k_norm_g", "swa_q_norm_g", "swa_k_norm_g")], axis=1)
    bfs = np.concatenate([np.broadcast_to(np.asarray(inputs["b_forget"][0], np.float32), (128, 16)),
                          np.broadcast_to(np.asarray(inputs["swa_sinks"][0], np.float32), (128, 16))], axis=1)
    cmat, invf = _host_consts()
    shared = {
        "wg1": f(inputs["ffn1_w_gate"][0]), "wu1": f(inputs["ffn1_w_up"][0]), "wd1": f(inputs["ffn1_w_down"][0]),
        "win": f(inputs["w_in"][0]), "wout": f(inputs["w_out"][0]),
        "wg2": f(inputs["ffn2_w_gate"][0]), "wu2": f(inputs["ffn2_w_up"][0]), "wd2": f(inputs["ffn2_w_down"][0]),
        "gcols": f(gcols), "qkg": f(qkg), "bfs": f(bfs), "cmat": cmat, "invf": invf,
    }
    xs = np.asarray(inputs["x"], np.float32)
    ps = np.asarray(inputs["positions"], np.int32)
    maps = []
    for b in range(NCORES):
        m = dict(shared)
        m["x"] = np.ascontiguousarray(xs[b])
        m["pos"] = np.ascontiguousarray(ps[b][None, :])
        maps.append(m)
    return maps


def kernel(**inputs):
    nc = build(stage=3, debug=False)
    maps = _in_maps(inputs)
    res = run_bass_kernel_spmd(nc, maps, core_ids=list(range(NCORES)))
    return np.stack([np.asarray(r["out"], np.float32) for r in res.results], axis=0)
```
